# Optimizing a Trainium2 kernel written in Bass

```python
import math
import jax, jax.numpy as jnp
from jax import lax
import numpy as np

D_MODEL = 2048
BATCH = 1
SEQ = 16384
DEPTH = 2

N_A_LAYERS = DEPTH // 2
N_B_LAYERS = DEPTH - N_A_LAYERS
EPS = 1e-6

GLA_HEADS = 4
GLA_KEY_DIM = D_MODEL // 2
GLA_VAL_DIM = D_MODEL
GLA_HEAD_K = GLA_KEY_DIM // GLA_HEADS
GLA_HEAD_V = GLA_VAL_DIM // GLA_HEADS
GLA_GATE_RANK = 16
GLA_GATE_NORMALIZER = 16.0
GLA_CHUNK = 64

MOBA_HEADS = 16
MOBA_HEAD_DIM = D_MODEL // MOBA_HEADS
MOBA_BLOCK = 256
MOBA_TOPK = 3
MOBA_Q_CHUNK = 32

REL_BUCKETS = 32
REL_MAX_DISTANCE = 128

FFN_HIDDEN = ((8 * D_MODEL // 3 + 255) // 256) * 256
CONV_WIDTH = 3

kernel_name = "yoco_gla_moba_convffn_t5bias"


def rms_norm(x, w, eps=EPS):
    xf = x.astype(jnp.float32)
    y = xf * lax.rsqrt(jnp.mean(xf * xf, axis=-1, keepdims=True) + eps)
    return (y * w.astype(jnp.float32)).astype(x.dtype)


def rel_bucket(dist):
    max_exact = REL_BUCKETS // 2
    n = jnp.maximum(dist, 0)
    large = max_exact + (jnp.log(jnp.maximum(n, 1).astype(jnp.float32) / max_exact)
                         / math.log(REL_MAX_DISTANCE / max_exact)
                         * (REL_BUCKETS - max_exact)).astype(jnp.int32)
    large = jnp.minimum(large, REL_BUCKETS - 1)
    return jnp.where(n < max_exact, n, large)


def gla_mixer(x, norm_w, w_in, gk_w1, gk_w2, gk_b, o_norm_w, w_out):
    B, S, _ = x.shape
    H, dk, dv, C = GLA_HEADS, GLA_HEAD_K, GLA_HEAD_V, GLA_CHUNK
    nc = S // C
    f32 = jnp.float32
    xn = rms_norm(x, norm_w)
    proj = xn @ w_in
    q, k, v, g = jnp.split(proj, [GLA_KEY_DIM, 2 * GLA_KEY_DIM, 2 * GLA_KEY_DIM + GLA_VAL_DIM], axis=-1)
    gk = (xn @ gk_w1) @ gk_w2 + gk_b
    log_a = jax.nn.log_sigmoid(gk.astype(f32)) / GLA_GATE_NORMALIZER

    def to_chunks(t, d):
        return t.astype(f32).reshape(B, nc, C, H, d).transpose(0, 3, 1, 2, 4)

    q = to_chunks(q, dk) * (dk ** -0.5)
    k = to_chunks(k, dk)
    v = to_chunks(v, dv)
    b = jnp.cumsum(to_chunks(log_a, dk), axis=3)
    b_last = b[:, :, :, -1:, :]
    q_dec = q * jnp.exp(b)
    k_inv = k * jnp.exp(-b)
    k_end = k * jnp.exp(b_last - b)
    causal = jnp.tril(jnp.ones((C, C), dtype=bool))
    attn = jnp.where(causal, jnp.einsum('bhnqk,bhnsk->bhnqs', q_dec, k_inv), 0.0)
    o_intra = jnp.einsum('bhnqs,bhnsv->bhnqv', attn, v)
    chunk_decay = jnp.exp(b_last[:, :, :, 0, :])

    def step(state, inp):
        q_c, k_c, v_c, d_c = inp
        o_c = jnp.einsum('bhqk,bhkv->bhqv', q_c, state)
        state = state * d_c[..., None] + jnp.einsum('bhsk,bhsv->bhkv', k_c, v_c)
        return state, o_c

    xs = (jnp.moveaxis(q_dec, 2, 0), jnp.moveaxis(k_end, 2, 0),
          jnp.moveaxis(v, 2, 0), jnp.moveaxis(chunk_decay, 2, 0))
    _, o_inter = lax.scan(step, jnp.zeros((B, H, dk, dv), f32), xs)
    o = o_intra + jnp.moveaxis(o_inter, 0, 2)
    o = o.transpose(0, 2, 3, 1, 4).reshape(B, S, H, dv)
    o = rms_norm(o, o_norm_w) * jax.nn.silu(g.astype(f32)).reshape(B, S, H, dv)
    return o.reshape(B, S, GLA_VAL_DIM).astype(x.dtype) @ w_out


def conv_ffn(x, norm_w, w_up, conv_w, conv_b, w_down):
    S = x.shape[1]
    h = rms_norm(x, norm_w) @ w_up
    hp = jnp.pad(h, ((0, 0), (CONV_WIDTH - 1, 0), (0, 0)))
    acc = conv_b
    for j in range(CONV_WIDTH):
        acc = acc + conv_w[j] * hp[:, j:j + S]
    a, u = jnp.split(acc, 2, axis=-1)
    return (jax.nn.silu(a) * u) @ w_down


def shared_kv(x, norm_w, w_kv, k_norm_w):
    B, S, _ = x.shape
    H, Dh, BLK = MOBA_HEADS, MOBA_HEAD_DIM, MOBA_BLOCK
    nb = -(-S // BLK)
    s_pad = nb * BLK
    kv = rms_norm(x, norm_w) @ w_kv
    k, v = jnp.split(kv, 2, axis=-1)
    k = rms_norm(k.reshape(B, S, H, Dh), k_norm_w)
    v = v.reshape(B, S, H, Dh)
    pad = ((0, 0), (0, s_pad - S), (0, 0), (0, 0))
    k_blk = jnp.pad(k, pad).reshape(B, nb, BLK, H, Dh).transpose(0, 3, 1, 2, 4)
    v_blk = jnp.pad(v, pad).reshape(B, nb, BLK, H, Dh).transpose(0, 3, 1, 2, 4)
    k_mean = jnp.mean(k_blk.astype(jnp.float32), axis=3).astype(k_blk.dtype)
    return k_blk, v_blk, k_mean


def moba_mixer(x, norm_w, w_q, q_norm_w, w_out, k_blk, v_blk, k_mean, rel_bias):
    B, S, _ = x.shape
    H, Dh, BLK, QC = MOBA_HEADS, MOBA_HEAD_DIM, MOBA_BLOCK, MOBA_Q_CHUNK
    nb = k_blk.shape[2]
    topk = min(MOBA_TOPK, nb)
    q = (rms_norm(x, norm_w) @ w_q).reshape(B, S, H, Dh)
    q = (rms_norm(q, q_norm_w) * (Dh ** -0.5)).transpose(0, 2, 1, 3)
    bias_table = rel_bias.T
    b_idx = jnp.arange(B)[:, None, None, None]
    h_idx = jnp.arange(H)[None, :, None, None]
    blk_ids = jnp.arange(nb)
    offs = jnp.arange(BLK)

    def attend(c):
        start = c * QC
        q_c = lax.dynamic_slice_in_dim(q, start, QC, axis=2)
        t = start + jnp.arange(QC)
        own = start // BLK
        gate = jnp.einsum('bhqd,bhnd->bhqn', q_c, k_mean).astype(jnp.float32)
        gate = jnp.where(blk_ids < own, gate, -jnp.inf)
        _, sel = lax.top_k(gate, topk)
        sel_ok = jnp.arange(topk) < own
        k_sel = k_blk[b_idx, h_idx, sel]
        v_sel = v_blk[b_idx, h_idx, sel]
        pos_sel = sel[..., None] * BLK + offs
        s_sel = jnp.einsum('bhqd,bhqjkd->bhqjk', q_c, k_sel).astype(jnp.float32)
        s_sel = s_sel + bias_table[h_idx[..., None], rel_bucket(t[:, None, None] - pos_sel)]
        s_sel = jnp.where(sel_ok[:, None], s_sel, -jnp.inf)
        k_own = lax.dynamic_index_in_dim(k_blk, own, axis=2, keepdims=False)
        v_own = lax.dynamic_index_in_dim(v_blk, own, axis=2, keepdims=False)
        dist_own = t[:, None] - (own * BLK + offs)[None, :]
        s_own = jnp.einsum('bhqd,bhkd->bhqk', q_c, k_own).astype(jnp.float32)
        s_own = s_own + bias_table[:, rel_bucket(dist_own)][None]
        s_own = jnp.where(dist_own >= 0, s_own, -jnp.inf)
        scores = jnp.concatenate([s_sel.reshape(B, H, QC, topk * BLK), s_own], axis=-1)
        p = jax.nn.softmax(scores, axis=-1).astype(v_blk.dtype)
        p_sel = p[..., :topk * BLK].reshape(B, H, QC, topk, BLK)
        p_own = p[..., topk * BLK:]
        return (jnp.einsum('bhqjk,bhqjkd->bhqd', p_sel, v_sel)
                + jnp.einsum('bhqk,bhkd->bhqd', p_own, v_own))

    o = lax.map(attend, jnp.arange(S // QC))
    o = o.transpose(1, 0, 3, 2, 4).reshape(B, S, H * Dh)
    return o @ w_out


def setup_inputs(seed: int = 0) -> dict:
    key = jax.random.key(seed)
    ks = jax.random.split(key, 21)
    f32 = jnp.float32

    def nrm(k, shape, scale):
        return jax.random.normal(k, shape, f32) * scale

    def gain(k, shape):
        return 1.0 + 0.02 * jax.random.normal(k, shape, f32)

    D, KD, VD, R = D_MODEL, GLA_KEY_DIM, GLA_VAL_DIM, GLA_GATE_RANK
    HD = MOBA_HEADS * MOBA_HEAD_DIM
    F = FFN_HIDDEN
    return {
        "x": nrm(ks[0], (BATCH, SEQ, D), 1.0),
        "gla_norm": gain(ks[1], (N_A_LAYERS, D)),
        "gla_w_in": nrm(ks[2], (N_A_LAYERS, D, 2 * KD + 2 * VD), D ** -0.5),
        "gla_gk_w1": nrm(ks[3], (N_A_LAYERS, D, R), D ** -0.5),
        "gla_gk_w2": nrm(ks[4], (N_A_LAYERS, R, KD), R ** -0.5),
        "gla_gk_b": nrm(ks[5], (N_A_LAYERS, KD), 0.1),
        "gla_o_norm": gain(ks[6], (N_A_LAYERS, GLA_HEAD_V)),
        "gla_w_out": nrm(ks[7], (N_A_LAYERS, VD, D), VD ** -0.5),
        "kv_norm": gain(ks[8], (D,)),
        "kv_w": nrm(ks[9], (D, 2 * HD), D ** -0.5),
        "k_norm_w": gain(ks[10], (MOBA_HEAD_DIM,)),
        "moba_norm": gain(ks[11], (N_B_LAYERS, D)),
        "moba_w_q": nrm(ks[12], (N_B_LAYERS, D, HD), D ** -0.5),
        "moba_q_norm": gain(ks[13], (N_B_LAYERS, MOBA_HEAD_DIM)),
        "moba_w_out": nrm(ks[14], (N_B_LAYERS, HD, D), HD ** -0.5),
        "rel_bias": nrm(ks[15], (REL_BUCKETS, MOBA_HEADS), 0.2),
        "ffn_norm": gain(ks[16], (DEPTH, D)),
        "ffn_w_up": nrm(ks[17], (DEPTH, D, 2 * F), D ** -0.5),
        "ffn_conv_w": nrm(ks[18], (DEPTH, CONV_WIDTH, 2 * F), CONV_WIDTH ** -0.5),
        "ffn_conv_b": nrm(ks[19], (DEPTH, 2 * F), 0.02),
        "ffn_w_down": nrm(ks[20], (DEPTH, F, D), F ** -0.5),
    }


def reference(x, gla_norm, gla_w_in, gla_gk_w1, gla_gk_w2, gla_gk_b, gla_o_norm, gla_w_out,
              kv_norm, kv_w, k_norm_w, moba_norm, moba_w_q, moba_q_norm, moba_w_out, rel_bias,
              ffn_norm, ffn_w_up, ffn_conv_w, ffn_conv_b, ffn_w_down):
    h = x
    k_blk = v_blk = k_mean = None
    for layer in range(DEPTH):
        if layer < N_A_LAYERS:
            i = layer
            h = h + gla_mixer(h, gla_norm[i], gla_w_in[i], gla_gk_w1[i], gla_gk_w2[i],
                              gla_gk_b[i], gla_o_norm[i], gla_w_out[i])
        else:
            if layer == N_A_LAYERS:
                k_blk, v_blk, k_mean = shared_kv(h, kv_norm, kv_w, k_norm_w)
            j = layer - N_A_LAYERS
            h = h + moba_mixer(h, moba_norm[j], moba_w_q[j], moba_q_norm[j], moba_w_out[j],
                               k_blk, v_blk, k_mean, rel_bias)
        h = h + conv_ffn(h, ffn_norm[layer], ffn_w_up[layer], ffn_conv_w[layer],
                         ffn_conv_b[layer], ffn_w_down[layer])
    return h
```

```python
import contextlib
import numpy as np
import concourse.bass as bass
import concourse.mybir as mybir
from concourse.bass_utils import run_bass_kernel_spmd
F32 = mybir.dt.float32; BF16 = mybir.dt.bfloat16
AF = mybir.ActivationFunctionType; ALU = mybir.AluOpType
EPS = 1e-6

class _Q:
    def __init__(self, name, eng, sem, inc, ename):
        self.name = name; self.eng = eng; self.sem = sem; self.inc = inc; self.count = 0; self.ename = ename

NS_DMA = 8

class Sched:
    def __init__(self, nc, stack):
        self.nc = nc; self.stack = stack; self.epoch = 0
        def sem(n): return stack.enter_context(nc.semaphore(n))
        self.q = {
            'pe': _Q('pe', nc.tensor, sem('s_pe'), 1, 'pe'),
            'act': _Q('act', nc.scalar, sem('s_act'), 1, 'act'),
            'dve': _Q('dve', nc.vector, sem('s_dve'), 1, 'dve'),
            'pool': _Q('pool', nc.gpsimd, sem('s_pool'), 1, 'pool'),
        }
        self.rr = {}
        for qn, eng, en in (('dsp', nc.sync, 'sp'), ('dact', nc.scalar, 'act'), ('dpool', nc.gpsimd, 'pool')):
            self.rr[qn] = 0
            for k in range(NS_DMA):
                self.q[f'{qn}#{k}'] = _Q(f'{qn}#{k}', eng, sem(f's_{qn}_{k}'), 16, en)
        self.engs = {'pe': nc.tensor, 'act': nc.scalar, 'dve': nc.vector, 'pool': nc.gpsimd, 'sp': nc.sync}
        self.lastw = {}; self.reads = {}
        self.seen = {e: {} for e in self.engs}
        self.n_inst = 0; self.n_wait = 0
    def issue(self, qname, fn, reads=(), writes=()):
        if qname in self.rr:
            k = self.rr[qname]; self.rr[qname] = (k + 1) % NS_DMA
            qname = f'{qname}#{k}'
        Qo = self.q[qname]; E = Qo.ename
        deps = {}
        def add(qn, s):
            if qn == qname and qname == 'pe': return
            if deps.get(qn, 0) < s: deps[qn] = s
        for r in reads:
            for qn, s in self.lastw.get(r, {}).items(): add(qn, s)
        for w in writes:
            for qn, s in self.lastw.get(w, {}).items(): add(qn, s)
            for qn, s in self.reads.get(w, {}).items(): add(qn, s)
        seen = self.seen[E]
        for qn, s in deps.items():
            if seen.get(qn, 0) >= s: continue
            dq = self.q[qn]
            Qo.eng.wait_ge(dq.sem, s * dq.inc); self.n_wait += 1
            seen[qn] = s
        ins = fn(Qo.eng)
        Qo.count += 1
        ins.then_inc(Qo.sem, Qo.inc)
        self.n_inst += 1
        seq = Qo.count
        for w in writes:
            self.lastw[w] = {qname: seq}; self.reads[w] = {}
        for r in reads:
            d = self.reads.setdefault(r, {})
            if d.get(qname, 0) < seq: d[qname] = seq
        return ins
    def barrier(self, new_sems=False):
        for en, eng in self.engs.items():
            for qn, dq in self.q.items():
                if dq.count == 0 or self.seen[en].get(qn, 0) >= dq.count: continue
                eng.wait_ge(dq.sem, dq.count * dq.inc); self.seen[en][qn] = dq.count
        self.lastw = {}; self.reads = {}
        if not new_sems: return
        self.epoch += 1
        for qn, dq in self.q.items():
            if dq.count == 0 or dq.inc != 1: continue
            dq.sem = self.stack.enter_context(self.nc.semaphore(f"s_{qn}_{self.epoch}"))
            dq.count = 0
            for en in self.engs: self.seen[en].pop(qn, None)
    def finish(self):
        for qn, dq in self.q.items():
            if dq.count: self.nc.sync.wait_ge(dq.sem, dq.count * dq.inc)

class Ctx:
    def __init__(self, name="k"):
        self.nc = bass.Bass("TRN2", target_bir_lowering=False)
        self.stack = contextlib.ExitStack()
        self.S = Sched(self.nc, self.stack)
        self.uid = 0
    def dram(self, name, shape, dt, kind):
        return self.nc.dram_tensor(name, list(shape), dt, kind=kind).ap()
    def sb(self, shape, dt, stack=None, name=None):
        self.uid += 1
        return (stack or self.stack).enter_context(self.nc.sbuf_tensor(name or f"sb{self.uid}", list(shape), dt))
    def ps(self, shape, dt, stack=None, name=None):
        self.uid += 1
        return (stack or self.stack).enter_context(self.nc.psum_tensor(name or f"ps{self.uid}", list(shape), dt))
    def done(self):
        self.S.finish(); self.stack.close(); return self.nc

def run(nc, in_maps):
    res = run_bass_kernel_spmd(nc, in_maps, core_ids=list(range(8)))
    return res.results

T = 2048
TT = 512

def frontend(C, aT, KC, cols, anT, nw_sb=None, G=None, gT=None, scale_extra=None):
    nc, S = C.nc, C.S
    with contextlib.ExitStack() as st:
        a_sb = C.sb([128, KC, TT], F32, st)
        g_sb = C.sb([128, KC, TT], F32, st) if gT is not None else None
        sq = [C.sb([128, TT], BF16, st) for _ in range(2)]
        ones = C.sb([128, 128], BF16, st)
        ngr = (KC // G) if G else 0
        rt = [C.sb([128, TT], F32, st) for _ in range(max(ngr, 1))]
        rstd = [C.sb([128, TT], F32, st) for _ in range(max(ngr, 1))]
        tmp = [C.sb([128, TT], F32, st) for _ in range(2)]
        pss = C.ps([128, 4, TT], F32, st)
        S.issue('dve', lambda e: e.memset(ones[:], 1.0), writes=['ones'])
        aTv = aT.rearrange("(c p) t -> p c t", p=128)
        gTv = gT.rearrange("(c p) t -> p c t", p=128) if gT is not None else None
        for (c0, w) in cols:
            S.issue('dsp', lambda e: e.dma_start(out=a_sb[:, :, 0:w], in_=aTv[:, :, c0:c0 + w]), writes=['a_sb'])
            if gT is not None:
                S.issue('dact', lambda e: e.dma_start(out=g_sb[:, :, 0:w], in_=gTv[:, :, c0:c0 + w]), writes=['g_sb'])
            if G:
                for c in range(KC):
                    g = c // G
                    S.issue('act', lambda e: e.activation(out=sq[c % 2][:, 0:w], in_=a_sb[:, c, 0:w], func=AF.Square),
                            reads=['a_sb'], writes=[('sq', c % 2)])
                    S.issue('pe', lambda e: e.matmul(pss[:, g, 0:w], lhsT=ones[:], rhs=sq[c % 2][:, 0:w],
                                                     start=(c % G == 0), stop=(c % G == G - 1)),
                            reads=['ones', ('sq', c % 2)], writes=[('pss', g)])
                for g in range(ngr):
                    S.issue('act', lambda e: e.activation(out=rt[g][:, 0:w], in_=pss[:, g, 0:w], func=AF.Sqrt,
                                                          scale=1.0 / (G * 128), bias=EPS),
                            reads=[('pss', g)], writes=[('rt', g)])
                    S.issue('dve', lambda e: e.reciprocal(out=rstd[g][:, 0:w], in_=rt[g][:, 0:w]),
                            reads=[('rt', g)], writes=[('rstd', g)])
            for c in range(KC):
                if G:
                    g = c // G
                    if gT is None:
                        S.issue('dve', lambda e: e.scalar_tensor_tensor(out=anT[:, c, c0:c0 + w], in0=a_sb[:, c, 0:w],
                                scalar=nw_sb[:, c:c + 1], op0=ALU.mult, in1=rstd[g][:, 0:w], op1=ALU.mult),
                                reads=['a_sb', ('rstd', g), 'nw'], writes=[('anT', c, c0)])
                    else:
                        S.issue('dve', lambda e: e.scalar_tensor_tensor(out=tmp[0][:, 0:w], in0=a_sb[:, c, 0:w],
                                scalar=nw_sb[:, c:c + 1], op0=ALU.mult, in1=rstd[g][:, 0:w], op1=ALU.mult),
                                reads=['a_sb', ('rstd', g), 'nw'], writes=[('tmp', 0)])
                        S.issue('act', lambda e: e.activation(out=tmp[1][:, 0:w], in_=g_sb[:, c, 0:w], func=AF.Silu),
                                reads=['g_sb'], writes=[('tmp', 1)])
                        S.issue('dve', lambda e: e.tensor_tensor(out=anT[:, c, c0:c0 + w], in0=tmp[0][:, 0:w],
                                in1=tmp[1][:, 0:w], op=ALU.mult),
                                reads=[('tmp', 0), ('tmp', 1)], writes=[('anT', c, c0)])
                else:
                    q = 'act' if c % 2 else 'dve'
                    if q == 'act':
                        S.issue('act', lambda e: e.activation(out=anT[:, c, c0:c0 + w], in_=a_sb[:, c, 0:w], func=AF.Copy),
                                reads=['a_sb'], writes=[('anT', c, c0)])
                    else:
                        S.issue('dve', lambda e: e.tensor_copy(out=anT[:, c, c0:c0 + w], in_=a_sb[:, c, 0:w]),
                                reads=['a_sb'], writes=[('anT', c, c0)])
        S.barrier()

def build_normproj(KC, N, outs, norm=True, G=None, gate=False, resid=False, gla_gate=False):
    C = Ctx(); nc, S = C.nc, C.S
    aT = C.dram("aT", [KC * 128, T], F32, "ExternalInput")
    W = C.dram("W", [KC * 128, N], F32, "ExternalInput")
    nw = C.dram("nw", [128, KC], F32, "ExternalInput") if norm else None
    gT = C.dram("gT", [KC * 128, T], F32, "ExternalInput") if gate else None
    resT = C.dram("resT", [N, T], F32, "ExternalInput") if resid else None
    od = {}
    for (kind, col0, ncols, dt, name) in outs:
        od[name] = C.dram(name, [ncols, T] if kind == 'f' else [T, ncols], dt, "ExternalOutput")
    if gla_gate:
        w1 = C.dram("w1", [KC * 128, 16], F32, "ExternalInput")
        w2e = C.dram("w2e", [17, 1024], F32, "ExternalInput")
        la = C.dram("la", [T, 1024], F32, "ExternalOutput")
    anT = C.sb([128, KC, T], BF16)
    nw_sb = None
    if norm:
        nw_sb = C.sb([128, KC], F32)
        S.issue('dsp', lambda e: e.dma_start(out=nw_sb[:], in_=nw), writes=['nw'])
    frontend(C, aT, KC, [(i * TT, TT) for i in range(T // TT)], anT, nw_sb, G if norm else None, gT)
    Wv = W.rearrange("(c p) n -> p c n", p=128)
    with contextlib.ExitStack() as st:
        wb = [C.sb([128, KC, 512], BF16, st) for _ in range(2)]
        stf = [C.sb([128, T], F32, st) for _ in range(2)]
        stt = [C.sb([128, 4, 512], F32, st) for _ in range(2)]
        sttb = [C.sb([128, 4, 512], BF16, st) for _ in range(2)]
        r_sb = [C.sb([128, TT], F32, st) for _ in range(4)]
        ps = C.ps([128, 6, 512], F32, st)
        cnt = dict(w=0, b=0, f=0, t=0, r=0, e=0)
        def evac(src, dst, reads, writes):
            cnt['e'] += 1
            if cnt['e'] % 2:
                S.issue('act', lambda e: e.activation(out=dst, in_=src, func=AF.Copy), reads=reads, writes=writes)
            else:
                S.issue('dve', lambda e: e.tensor_copy(out=dst, in_=src), reads=reads, writes=writes)
        if gla_gate:
            w1b = C.sb([128, KC, 16], BF16, st)
            rTe = C.sb([17, T], F32, st)
            w2s = C.sb([17, 1024], F32, st)
            S.issue('dpool', lambda e: e.dma_start(out=w1b[:], in_=w1.rearrange("(c p) n -> p c n", p=128)), writes=['w1b'])
            S.issue('dsp', lambda e: e.dma_start(out=w2s[:], in_=w2e), writes=['w2s'])
            S.issue('dve', lambda e: e.memset(rTe[:], 1.0), writes=['rTe'])
            for tt in range(T // TT):
                b = cnt['b'] % 6; cnt['b'] += 1
                for kc in range(KC):
                    S.issue('pe', lambda e: e.matmul(ps[0:16, b, :], lhsT=w1b[:, kc, :], rhs=anT[:, kc, tt * TT:(tt + 1) * TT],
                                                     start=(kc == 0), stop=(kc == KC - 1)), reads=['w1b'], writes=[('ps', b)])
                S.issue('act', lambda e: e.activation(out=rTe[0:16, tt * TT:(tt + 1) * TT], in_=ps[0:16, b, :], func=AF.Copy),
                        reads=[('ps', b)], writes=['rTe'])
            lav = la.rearrange("(i p) n -> p i n", p=128)
            for ti in range(T // 128):
                sl = (ti // 2) % 2
                for hf in range(2):
                    b = cnt['b'] % 6; cnt['b'] += 1
                    S.issue('pe', lambda e: e.matmul(ps[:, b, :], lhsT=rTe[:, ti * 128:(ti + 1) * 128], rhs=w2s[:, hf * 512:(hf + 1) * 512],
                                                     start=True, stop=True), reads=['rTe', 'w2s'], writes=[('ps', b)])
                    k = cnt['r'] % 4; cnt['r'] += 1
                    S.issue('act', lambda e: e.activation(out=r_sb[k][:], in_=ps[:, b, :], func=AF.Exp, scale=-1.0),
                            reads=[('ps', b)], writes=[('r_sb', k)])
                    S.issue('act', lambda e: e.activation(out=r_sb[k][:], in_=r_sb[k][:], func=AF.Ln, bias=1.0),
                            reads=[('r_sb', k)], writes=[('r_sb', k)])
                    S.issue('dve', lambda e: e.tensor_scalar(out=stt[sl][:, (ti % 2) * 2 + hf, :], in0=r_sb[k][:], scalar1=-1.0 / 16.0,
                                                             scalar2=None, op0=ALU.mult),
                            reads=[('r_sb', k)], writes=[('stt', sl)])
                if ti % 2 == 1:
                    S.issue('dsp', lambda e: e.dma_start(out=lav[:, ti - 1:ti + 1, :], in_=stt[sl][:].rearrange("p (i h) n -> p i (h n)", h=2)),
                            reads=[('stt', sl)], writes=[('la', ti)])
        for (kind, col0, ncols, dt, name) in outs:
            o = od[name]
            for cb in range(0, ncols, 512):
                wcols = min(512, ncols - cb)
                wi = cnt['w'] % 2; cnt['w'] += 1
                S.issue('dpool', lambda e: e.dma_start(out=wb[wi][:, :, 0:wcols], in_=Wv[:, :, col0 + cb:col0 + cb + wcols]),
                        writes=[('wb', wi)])
                if kind == 'f':
                    for j in range(wcols // 128):
                        fi = cnt['f'] % 2; cnt['f'] += 1
                        row0 = cb + j * 128
                        for tt in range(T // TT):
                            b = cnt['b'] % 6; cnt['b'] += 1
                            for kc in range(KC):
                                S.issue('pe', lambda e: e.matmul(ps[:, b, :], lhsT=wb[wi][:, kc, j * 128:(j + 1) * 128],
                                                                 rhs=anT[:, kc, tt * TT:(tt + 1) * TT], start=(kc == 0), stop=(kc == KC - 1)),
                                        reads=[('wb', wi)], writes=[('ps', b)])
                            if resid:
                                k = cnt['r'] % 4; cnt['r'] += 1
                                S.issue('dact', lambda e: e.dma_start(out=r_sb[k][:], in_=resT[col0 + row0:col0 + row0 + 128, tt * TT:(tt + 1) * TT]),
                                        writes=[('r_sb', k)])
                                S.issue('dve', lambda e: e.tensor_tensor(out=stf[fi][:, tt * TT:(tt + 1) * TT], in0=ps[:, b, :], in1=r_sb[k][:], op=ALU.add),
                                        reads=[('ps', b), ('r_sb', k)], writes=[('stf', fi)])
                            else:
                                evac(ps[:, b, :], stf[fi][:, tt * TT:(tt + 1) * TT], [('ps', b)], [('stf', fi)])
                        S.issue('dsp', lambda e: e.dma_start(out=o[row0:row0 + 128, :], in_=stf[fi][:]), reads=[('stf', fi)], writes=[(name, 'f', row0)])
                else:
                    ov = o.rearrange("(i p) n -> p i n", p=128)
                    stg = sttb if dt == BF16 else stt
                    sk = 'sttb' if dt == BF16 else 'stt'
                    for ti in range(T // 128):
                        ti_slot = cnt['t'] % 2
                        b = cnt['b'] % 6; cnt['b'] += 1
                        for kc in range(KC):
                            S.issue('pe', lambda e: e.matmul(ps[:, b, 0:wcols], lhsT=anT[:, kc, ti * 128:(ti + 1) * 128], rhs=wb[wi][:, kc, 0:wcols],
                                                             start=(kc == 0), stop=(kc == KC - 1)), reads=[('wb', wi)], writes=[('ps', b)])
                        evac(ps[:, b, 0:wcols], stg[ti_slot][:, ti % 4, 0:wcols], [('ps', b)], [(sk, ti_slot)])
                        if ti % 4 == 3:
                            S.issue('dsp', lambda e: e.dma_start(out=ov[:, ti - 3:ti + 1, cb:cb + wcols], in_=stg[ti_slot][:, :, 0:wcols]),
                                    reads=[(sk, ti_slot)], writes=[(name, 't', cb, ti)])
                            cnt['t'] += 1
    return C.done()

SEQ = 16384

def build_gla_rec():
    C = Ctx(); nc, S = C.nc, C.S
    qT = C.dram("qT", [256, SEQ], F32, "ExternalInput")
    kT = C.dram("kT", [256, SEQ], F32, "ExternalInput")
    kk = C.dram("k", [SEQ, 256], F32, "ExternalInput")
    la = C.dram("la", [SEQ, 256], F32, "ExternalInput")
    vv = C.dram("v", [SEQ, 256], BF16, "ExternalInput")
    Uc = C.dram("U", [128, 128], F32, "ExternalInput")
    Lc = C.dram("L", [128, 128], F32, "ExternalInput")
    oT = C.dram("oT", [256, SEQ], F32, "ExternalOutput")
    qTv = qT.rearrange("(c p) t -> p c t", p=128); kTv = kT.rearrange("(c p) t -> p c t", p=128)
    kv_ = kk.rearrange("(i p) n -> p i n", p=128); lav = la.rearrange("(i p) n -> p i n", p=128)
    vvv = vv.rearrange("(i p) n -> p i n", p=128); oTv = oT.rearrange("(c p) t -> p c t", p=128)
    U = C.sb([128, 128], F32); L = C.sb([128, 128], F32)
    S.issue('dsp', lambda e: e.dma_start(out=U[:], in_=Uc), writes=['U'])
    S.issue('dsp', lambda e: e.dma_start(out=L[:], in_=Lc), writes=['L'])
    q4 = [C.sb([128, 2, 512], F32) for _ in range(2)]
    k4T = [C.sb([128, 2, 512], F32) for _ in range(2)]
    k4 = [C.sb([128, 4, 256], F32) for _ in range(2)]
    la4 = [C.sb([128, 4, 256], F32) for _ in range(2)]
    v4 = [C.sb([128, 4, 256], BF16) for _ in range(2)]
    ost = [C.sb([128, 2, 512], F32) for _ in range(2)]
    Ep = [C.sb([128, 2, 128], F32) for _ in range(2)]
    Em = [C.sb([128, 2, 128], F32) for _ in range(2)]
    Er = [C.sb([128, 256], F32) for _ in range(2)]
    qd = [C.sb([128, 2, 128], BF16) for _ in range(2)]
    ki = [C.sb([128, 2, 128], BF16) for _ in range(2)]
    ke = [C.sb([128, 256], BF16) for _ in range(2)]
    am = [C.sb([128, 128], BF16) for _ in range(2)]
    St = C.sb([128, 2, 256], F32); Sb = C.sb([128, 2, 256], BF16)
    S.issue('dve', lambda e: e.memset(St[:], 0.0), writes=['St0', 'St1'])
    S.issue('dve', lambda e: e.memset(Sb[:], 0.0), writes=['Sb0', 'Sb1'])
    pb = [C.ps([128, 512], F32) for _ in range(2)]
    prem = C.ps([128, 512], F32); pat = C.ps([128, 512], F32)
    po = [C.ps([128, 512], F32) for _ in range(2)]
    pkv = [C.ps([128, 512], F32) for _ in range(2)]
    for g in range(SEQ // 512):
        gb = g % 2
        ts = slice(g * 512, (g + 1) * 512)
        S.issue('dsp', lambda e: e.dma_start(out=q4[gb][:], in_=qTv[:, :, ts]), writes=[('q4', gb)])
        S.issue('dsp', lambda e: e.dma_start(out=k4T[gb][:], in_=kTv[:, :, ts]), writes=[('k4T', gb)])
        S.issue('dact', lambda e: e.dma_start(out=k4[gb][:], in_=kv_[:, 4 * g:4 * g + 4, :]), writes=[('k4', gb)])
        S.issue('dact', lambda e: e.dma_start(out=la4[gb][:], in_=lav[:, 4 * g:4 * g + 4, :]), writes=[('la4', gb)])
        S.issue('dact', lambda e: e.dma_start(out=v4[gb][:], in_=vvv[:, 4 * g:4 * g + 4, :]), writes=[('v4', gb)])
        for i in range(4):
            n = 4 * g + i; b = n % 2
            cs = slice(i * 128, (i + 1) * 128)
            for kc in range(2):
                S.issue('pe', lambda e: e.matmul(pb[b][:, kc * 128:(kc + 1) * 128], lhsT=la4[gb][:, i, kc * 128:(kc + 1) * 128], rhs=U[:],
                                                 start=True, stop=True), reads=[('la4', gb), 'U'], writes=[('pb', b)])
            S.issue('pe', lambda e: e.matmul(prem[:, 0:256], lhsT=L[:], rhs=la4[gb][:, i, :], start=True, stop=True),
                    reads=[('la4', gb), 'L'], writes=['prem'])
            S.issue('act', lambda e: e.activation(out=Ep[b][:].rearrange("p c t -> p (c t)"), in_=pb[b][:, 0:256], func=AF.Exp),
                    reads=[('pb', b)], writes=[('Ep', b)])
            S.issue('act', lambda e: e.activation(out=Em[b][:].rearrange("p c t -> p (c t)"), in_=pb[b][:, 0:256], func=AF.Exp, scale=-1.0),
                    reads=[('pb', b)], writes=[('Em', b)])
            S.issue('act', lambda e: e.activation(out=Er[b][:], in_=prem[:, 0:256], func=AF.Exp),
                    reads=['prem'], writes=[('Er', b)])
            S.issue('dve', lambda e: e.scalar_tensor_tensor(out=qd[b][:], in0=q4[gb][:, :, cs], scalar=1.0 / 16.0, op0=ALU.mult,
                                                            in1=Ep[b][:], op1=ALU.mult),
                    reads=[('q4', gb), ('Ep', b)], writes=[('qd', b)])
            S.issue('dve', lambda e: e.tensor_tensor(out=ki[b][:], in0=k4T[gb][:, :, cs], in1=Em[b][:], op=ALU.mult),
                    reads=[('k4T', gb), ('Em', b)], writes=[('ki', b)])
            S.issue('pool', lambda e: e.tensor_tensor(out=ke[b][:], in0=k4[gb][:, i, :], in1=Er[b][:], op=ALU.mult),
                    reads=[('k4', gb), ('Er', b)], writes=[('ke', b)])
            for kc in range(2):
                S.issue('pe', lambda e: e.matmul(pat[:, 0:128], lhsT=ki[b][:, kc, :], rhs=qd[b][:, kc, :], start=(kc == 0), stop=(kc == 1)),
                        reads=[('ki', b), ('qd', b)], writes=['pat'])
            S.issue('dve', lambda e: e.tensor_tensor(out=am[b][:], in0=pat[:, 0:128], in1=U[:], op=ALU.mult),
                    reads=['pat', 'U'], writes=[('am', b)])
            for vc in range(2):
                S.issue('pe', lambda e: e.matmul(po[b][:, vc * 128:(vc + 1) * 128], lhsT=v4[gb][:, i, vc * 128:(vc + 1) * 128], rhs=am[b][:],
                                                 start=True, stop=False), reads=[('v4', gb), ('am', b)], writes=[('po', b)])
                for kc in range(2):
                    S.issue('pe', lambda e: e.matmul(po[b][:, vc * 128:(vc + 1) * 128], lhsT=Sb[:, kc, vc * 128:(vc + 1) * 128], rhs=qd[b][:, kc, :],
                                                     start=False, stop=(kc == 1)), reads=[f'Sb{kc}', ('qd', b)], writes=[('po', b)])
            S.issue('act', lambda e: e.activation(out=ost[gb][:, :, cs], in_=po[b][:, 0:256].rearrange("p (c t) -> p c t", c=2), func=AF.Copy),
                    reads=[('po', b)], writes=[('ost', gb)])
            for kc in range(2):
                S.issue('pe', lambda e: e.matmul(pkv[b][:, kc * 256:(kc + 1) * 256], lhsT=ke[b][:, kc * 128:(kc + 1) * 128], rhs=v4[gb][:, i, :],
                                                 start=True, stop=True), reads=[('ke', b), ('v4', gb)], writes=[('pkv', b)])
            for kc in range(2):
                S.issue('dve', lambda e: e.scalar_tensor_tensor(out=St[:, kc, :], in0=St[:, kc, :], scalar=Ep[b][:, kc, 127:128], op0=ALU.mult,
                                                                in1=pkv[b][:, kc * 256:(kc + 1) * 256], op1=ALU.add),
                        reads=[f'St{kc}', ('Ep', b), ('pkv', b)], writes=[f'St{kc}'])
                S.issue('pool', lambda e: e.tensor_copy(out=Sb[:, kc, :], in_=St[:, kc, :]), reads=[f'St{kc}'], writes=[f'Sb{kc}'])
        S.issue('dsp', lambda e: e.dma_start(out=oTv[:, :, ts], in_=ost[gb][:]), reads=[('ost', gb)], writes=[('oT', g)])
    return C.done()

FH = 5632
NFC = FH // 128

def build_ffn():
    C = Ctx(); nc, S = C.nc, C.S
    aTe = C.dram("aTe", [2048, T + 2], F32, "ExternalInput")
    nw = C.dram("nw", [128, 16], F32, "ExternalInput")
    Wup = C.dram("Wup", [2048, 2 * FH], F32, "ExternalInput")
    cwd = C.dram("cw", [128, 2 * NFC, 3], F32, "ExternalInput")
    cbd = C.dram("cb", [128, 2 * NFC], F32, "ExternalInput")
    Wdn = C.dram("Wdn", [FH, 2048], F32, "ExternalInput")
    hT = C.dram("hT", [2048, T], F32, "ExternalOutput")
    actT = C.dram("actT", [FH, T], BF16, "Internal")
    nw_sb = C.sb([128, 16], F32); cw = C.sb([128, 2 * NFC, 3], F32); cb = C.sb([128, 2 * NFC], F32)
    S.issue('dsp', lambda e: e.dma_start(out=nw_sb[:], in_=nw), writes=['nw'])
    S.issue('dsp', lambda e: e.dma_start(out=cw[:], in_=cwd), writes=['cw'])
    S.issue('dsp', lambda e: e.dma_start(out=cb[:], in_=cbd), writes=['cw'])
    Wuv = Wup.rearrange("(c p) n -> p c n", p=128)
    with contextlib.ExitStack() as st1:
        hnT = C.sb([128, 16, T + 2], BF16, st1)
        frontend(C, aTe, 16, [(0, 2)] + [(2 + i * TT, TT) for i in range(T // TT)], hnT, nw_sb, 16)
        wb = [C.sb([128, 16, 256], BF16, st1) for _ in range(2)]
        hbuf = [[C.sb([128, T + 2], F32, st1) for _ in range(2)] for _ in range(2)]
        y = [C.sb([128, T], F32, st1) for _ in range(2)]
        actb = [C.sb([128, T], BF16, st1) for _ in range(2)]
        ps = C.ps([128, 7, 512], F32, st1)
        ph = C.ps([128, 512], F32, st1)
        nb = 0; ne = 0
        for fc in range(NFC):
            wi = fc % 2
            S.issue('dpool', lambda e: e.dma_start(out=wb[wi][:, :, 0:128], in_=Wuv[:, :, fc * 128:(fc + 1) * 128]), writes=[('wb', wi)])
            S.issue('dpool', lambda e: e.dma_start(out=wb[wi][:, :, 128:256], in_=Wuv[:, :, FH + fc * 128:FH + (fc + 1) * 128]), writes=[('wb', wi)])
            for s in range(2):
                hb = hbuf[s][wi]; hk = ('hbuf', s, wi)
                for kc in range(16):
                    S.issue('pe', lambda e: e.matmul(ph[:, s * 8:s * 8 + 2], lhsT=wb[wi][:, kc, s * 128:(s + 1) * 128], rhs=hnT[:, kc, 0:2],
                                                     start=(kc == 0), stop=(kc == 15)), reads=[('wb', wi)], writes=['ph'])
                S.issue('act', lambda e: e.activation(out=hb[:, 0:2], in_=ph[:, s * 8:s * 8 + 2], func=AF.Copy), reads=['ph'], writes=[hk])
                for tt in range(T // TT):
                    b = nb % 7; nb += 1
                    for kc in range(16):
                        S.issue('pe', lambda e: e.matmul(ps[:, b, :], lhsT=wb[wi][:, kc, s * 128:(s + 1) * 128],
                                                         rhs=hnT[:, kc, 2 + tt * TT:2 + (tt + 1) * TT], start=(kc == 0), stop=(kc == 15)),
                                reads=[('wb', wi)], writes=[('ps', b)])
                    ne += 1
                    if ne % 2:
                        S.issue('act', lambda e: e.activation(out=hb[:, 2 + tt * TT:2 + (tt + 1) * TT], in_=ps[:, b, :], func=AF.Copy),
                                reads=[('ps', b)], writes=[hk])
                    else:
                        S.issue('dve', lambda e: e.tensor_copy(out=hb[:, 2 + tt * TT:2 + (tt + 1) * TT], in_=ps[:, b, :]),
                                reads=[('ps', b)], writes=[hk])
                ch = s * NFC + fc
                S.issue('act', lambda e: e.activation(out=y[s][:], in_=hb[:, 2:T + 2], func=AF.Identity, scale=cw[:, ch, 2:3], bias=cb[:, ch:ch + 1]),
                        reads=[hk, 'cw'], writes=[('y', s)])
                S.issue('dve', lambda e: e.scalar_tensor_tensor(out=y[s][:], in0=hb[:, 1:T + 1], scalar=cw[:, ch, 1:2], op0=ALU.mult, in1=y[s][:], op1=ALU.add),
                        reads=[hk, 'cw', ('y', s)], writes=[('y', s)])
                S.issue('dve', lambda e: e.scalar_tensor_tensor(out=y[s][:], in0=hb[:, 0:T], scalar=cw[:, ch, 0:1], op0=ALU.mult, in1=y[s][:], op1=ALU.add),
                        reads=[hk, 'cw', ('y', s)], writes=[('y', s)])
            S.issue('act', lambda e: e.activation(out=y[0][:], in_=y[0][:], func=AF.Silu), reads=[('y', 0)], writes=[('y', 0)])
            S.issue('pool', lambda e: e.tensor_tensor(out=actb[wi][:], in0=y[0][:], in1=y[1][:], op=ALU.mult),
                    reads=[('y', 0), ('y', 1)], writes=[('actb', wi)])
            S.issue('dsp', lambda e: e.dma_start(out=actT[fc * 128:(fc + 1) * 128, :], in_=actb[wi][:]), reads=[('actb', wi)], writes=[('actT', fc)])
        S.barrier()
    actv = actT.rearrange("(c p) t -> p c t", p=128)
    Wdv = Wdn.rearrange("(c p) n -> p c n", p=128)
    with contextlib.ExitStack() as st2:
        act_sb = C.sb([128, NFC, 1024], BF16, st2)
        wd = [C.sb([128, NFC, 256], BF16, st2) for _ in range(2)]
        r_sb = [C.sb([128, TT], F32, st2) for _ in range(4)]
        stg = [C.sb([128, 1024], F32, st2) for _ in range(2)]
        ps = C.ps([128, 6, 512], F32, st2)
        nb = 0; nr = 0; nw_ = 0; ns = 0
        for th in range(2):
            for q4 in range(4):
                S.issue('dsp', lambda e: e.dma_start(out=act_sb[:, q4 * 11:(q4 + 1) * 11, :], in_=actv[:, q4 * 11:(q4 + 1) * 11, th * 1024:(th + 1) * 1024]),
                        reads=[('actT', fcx) for fcx in range(q4 * 11, (q4 + 1) * 11)], writes=['act_sb'])
            for cb_ in range(8):
                wi = nw_ % 2; nw_ += 1
                S.issue('dpool', lambda e: e.dma_start(out=wd[wi][:], in_=Wdv[:, :, cb_ * 256:(cb_ + 1) * 256]), writes=[('wd', wi)])
                for j in range(2):
                    si = ns % 2; ns += 1
                    row0 = cb_ * 256 + j * 128
                    for t2 in range(2):
                        b = nb % 6; nb += 1
                        for fcc in range(NFC):
                            S.issue('pe', lambda e: e.matmul(ps[:, b, :], lhsT=wd[wi][:, fcc, j * 128:(j + 1) * 128], rhs=act_sb[:, fcc, t2 * 512:(t2 + 1) * 512],
                                                             start=(fcc == 0), stop=(fcc == NFC - 1)), reads=[('wd', wi), 'act_sb'], writes=[('ps', b)])
                        k = nr % 4; nr += 1
                        c0 = 2 + th * 1024 + t2 * 512
                        S.issue('dact', lambda e: e.dma_start(out=r_sb[k][:], in_=aTe[row0:row0 + 128, c0:c0 + 512]), writes=[('r_sb', k)])
                        S.issue('dve', lambda e: e.tensor_tensor(out=stg[si][:, t2 * 512:(t2 + 1) * 512], in0=ps[:, b, :], in1=r_sb[k][:], op=ALU.add),
                                reads=[('ps', b), ('r_sb', k)], writes=[('stg', si)])
                    S.issue('dsp', lambda e: e.dma_start(out=hT[row0:row0 + 128, th * 1024:(th + 1) * 1024], in_=stg[si][:]), reads=[('stg', si)], writes=[('hT', th, row0)])
    return C.done()

def build_glapre():
    return build_normproj(16, 6144, [('f', 0, 1024, F32, 'qT'), ('f', 1024, 1024, F32, 'kT'), ('f', 4096, 2048, F32, 'ggT'),
                                     ('t', 1024, 1024, F32, 'k'), ('t', 2048, 2048, BF16, 'v')], norm=True, G=16, gla_gate=True)
def build_glapost():
    return build_normproj(16, 2048, [('f', 0, 2048, F32, 'hT')], norm=True, G=4, gate=True, resid=True)
def build_kvproj():
    return build_normproj(16, 4096, [('f', 0, 2048, F32, 'KT'), ('t', 2048, 2048, BF16, 'V')], norm=True, G=16)
def build_qproj():
    return build_normproj(16, 2048, [('f', 0, 2048, F32, 'QT')], norm=True, G=16)
def build_mobaout():
    return build_normproj(16, 2048, [('f', 0, 2048, F32, 'hT')], norm=False, resid=True)

NBLK = 64

def build_moba(nheads=2, nqg=SEQ // 512, ntiles=SEQ // 512, do_gate=True, skip=()):
    C = Ctx(); nc, S = C.nc, C.S
    QT = C.dram("QT", [256, SEQ], F32, "ExternalInput")
    KT = C.dram("KT", [256, SEQ], F32, "ExternalInput")
    V = C.dram("V", [SEQ, 256], BF16, "ExternalInput")
    qwd = C.dram("qw", [128, 2], F32, "ExternalInput")
    kwd = C.dram("kw", [128, 2], F32, "ExternalInput")
    c31d = C.dram("c31", [128, 2], F32, "ExternalInput")
    Tbd = C.dram("Tb", [128, 2, 6, 512], F32, "ExternalInput")
    identd = C.dram("ident", [128, 128], F32, "ExternalInput")
    seld = C.dram("sel", [128, 64, 128], BF16, "ExternalInput")
    oT = C.dram("oT", [256, SEQ], F32, "ExternalOutput")
    Vv = V.rearrange("(i p) n -> p i n", p=128)
    qw = C.sb([128, 2], F32); kw = C.sb([128, 2], F32); c31 = C.sb([128, 2], F32); nc31 = C.sb([128, 2], F32)
    ident = C.sb([128, 128], F32); sel = C.sb([128, 64, 128], BF16)
    ones_b = C.sb([128, 128], BF16)
    for dst, src, k in ((qw, qwd, 'qw'), (kw, kwd, 'kw'), (c31, c31d, 'c31'), (ident, identd, 'ident'), (sel, seld, 'sel')):
        S.issue('dsp', lambda e: e.dma_start(out=dst[:], in_=src), writes=[k])
    S.issue('dve', lambda e: e.memset(ones_b[:], 1.0), writes=['ones_b'])
    S.issue('dve', lambda e: e.tensor_scalar(out=nc31[:], in0=c31[:], scalar1=-1.0, scalar2=None, op0=ALU.mult), reads=['c31'], writes=['nc31'])
    QnT = C.sb([128, SEQ], BF16); KnT = C.sb([128, SEQ], BF16); Vb = C.sb([128, 128, 128], BF16)
    mbT = C.sb([128, SEQ], BF16); ET = C.sb([128, 6, 512], F32)
    S.issue('pool', lambda e: e.memset(mbT[64:128, :], 0.0), writes=['mbT'])
    for hh in range(nheads):
        rows = slice(hh * 128, (hh + 1) * 128)
        with contextlib.ExitStack() as st:
            xin = [C.sb([128, 512], F32, st) for _ in range(2)]
            sq = [C.sb([128, 512], BF16, st) for _ in range(2)]
            rt = [C.sb([128, 512], F32, st) for _ in range(2)]
            nf = [C.sb([128, 512], F32, st) for _ in range(2)]
            kmean = C.sb([128, NBLK], F32, st)
            gs = [C.sb([128, NBLK], F32, st) for _ in range(2)]
            m8 = [C.sb([128, 8], F32, st) for _ in range(2)]
            mb = [C.sb([128, 4, NBLK], F32, st) for _ in range(2)]
            pss = C.ps([128, 2, 512], F32, st)
            pg = C.ps([128, 4, 128], F32, st)
            pmt = C.ps([128, 2, 512], F32, st)
            S.issue('dsp', lambda e: e.dma_start(out=ET[:], in_=Tbd[:, hh, :, :]), writes=['ET'])
            if 'et' not in skip:
                S.issue('act', lambda e: e.activation(out=ET[:], in_=ET[:], func=AF.Exp, bias=nc31[:, hh:hh + 1]), reads=['ET', 'nc31'], writes=['ET'])
            for half in range(0 if 'vb' in skip else 8):
                S.issue('dact', lambda e: e.dma_start(out=Vb[:, half * 16:(half + 1) * 16, :], in_=Vv[:, half * 16:(half + 1) * 16, rows]), writes=['Vb'])
            for which in range(2):
                src = KT if which == 0 else QT
                wv = kw if which == 0 else qw
                for t in range(ntiles):
                    b = t % 2
                    ts = slice(t * 512, (t + 1) * 512)
                    S.issue('dsp', lambda e: e.dma_start(out=xin[b][:], in_=src[rows, ts]), writes=[('xin', b)])
                    S.issue('act', lambda e: e.activation(out=sq[b][:], in_=xin[b][:], func=AF.Square), reads=[('xin', b)], writes=[('sq', b)])
                    S.issue('pe', lambda e: e.matmul(pss[:, b, :], lhsT=ones_b[:], rhs=sq[b][:], start=True, stop=True),
                            reads=['ones_b', ('sq', b)], writes=[('pss', b)])
                    S.issue('act', lambda e: e.activation(out=rt[b][:], in_=pss[:, b, :], func=AF.Sqrt, scale=1.0 / 128, bias=EPS),
                            reads=[('pss', b)], writes=[('rt', b)])
                    S.issue('dve', lambda e: e.reciprocal(out=rt[b][:], in_=rt[b][:]), reads=[('rt', b)], writes=[('rt', b)])
                    S.issue('dve', lambda e: e.scalar_tensor_tensor(out=nf[b][:], in0=xin[b][:], scalar=wv[:, hh:hh + 1], op0=ALU.mult,
                                                                    in1=rt[b][:], op1=ALU.mult),
                            reads=[('xin', b), ('rt', b), 'qw', 'kw'], writes=[('nf', b)])
                    if which == 0:
                        S.issue('act', lambda e: e.activation(out=KnT[:, ts], in_=nf[b][:], func=AF.Copy), reads=[('nf', b)], writes=['KnT'])
                        if 'km' not in skip: S.issue('dve', lambda e: e.tensor_reduce(out=kmean[:, 2 * t:2 * t + 2], in_=nf[b][:].rearrange("p (a c) -> p a c", a=2),
                                                                 axis=mybir.AxisListType.X, op=ALU.add), reads=[('nf', b)], writes=['kmean'])
                    else:
                        S.issue('act', lambda e: e.activation(out=QnT[:, ts], in_=nf[b][:], func=AF.Copy, scale=128.0 ** -0.5),
                                reads=[('nf', b)], writes=['QnT'])
                        for i in range(4):
                            qt = 4 * t + i; own = qt // 2
                            if own >= 3 and do_gate:
                                S.issue('pe', lambda e: e.matmul(pg[:, i, 0:NBLK], lhsT=nf[b][:, i * 128:(i + 1) * 128], rhs=kmean[:], start=True, stop=True),
                                        reads=[('nf', b), 'kmean'], writes=['pg'])
                                g2 = i % 2
                                S.issue('pool', lambda e: e.memset(gs[g2][:], -1e30), writes=[('gs', g2)])
                                S.issue('dve', lambda e: e.tensor_copy(out=gs[g2][:, 0:own], in_=pg[:, i, 0:own]), reads=['pg'], writes=[('gs', g2)])
                                if 'mx' not in skip: S.issue('dve', lambda e: e.max(out=m8[g2][:], in_=gs[g2][:]), reads=[('gs', g2)], writes=[('m8', g2)])
                                else: S.issue('dve', lambda e: e.tensor_copy(out=m8[g2][:], in_=gs[g2][:, 0:8]), reads=[('gs', g2)], writes=[('m8', g2)])
                                if 'ts' not in skip: S.issue('dve', lambda e: e.tensor_scalar(out=gs[g2][:], in0=gs[g2][:], scalar1=m8[g2][:, 2:3], op0=ALU.is_ge,
                                                                         scalar2=30000.0, op1=ALU.mult), reads=[('gs', g2), ('m8', g2)], writes=[('gs', g2)])
                                S.issue('dve', lambda e: e.tensor_scalar(out=mb[b][:, i, :], in0=gs[g2][:], scalar1=-30000.0, scalar2=None, op0=ALU.add),
                                        reads=[('gs', g2)], writes=[('mb', b)])
                                S.issue('dve', lambda e: e.memset(mb[b][:, i, own:own + 1], 0.0), writes=[('mb', b)])
                            else:
                                S.issue('dve', lambda e: e.memset(mb[b][:, i, :], 0.0), writes=[('mb', b)])
                        for i in range(0 if 'tr' in skip else 4):
                            S.issue('pe', lambda e: e.transpose(pmt[0:64, b, i * 128:(i + 1) * 128], mb[b][:, i, :], ident[:]),
                                    reads=[('mb', b), 'ident'], writes=[('pmt', b)])
                        if 'tr' not in skip: S.issue('act', lambda e: e.activation(out=mbT[0:64, ts], in_=pmt[0:64, b, :], func=AF.Copy), reads=[('pmt', b)], writes=['mbT'])
                if which == 0:
                    S.issue('dve', lambda e: e.tensor_scalar(out=kmean[:], in0=kmean[:], scalar1=1.0 / 256, scalar2=None, op0=ALU.mult),
                            reads=['kmean'], writes=['kmean'])
            S.barrier(new_sems=True)
        with contextlib.ExitStack() as st:
            PT = [C.sb([128, 2, 512], BF16, st) for _ in range(3)]
            Pf = [C.sb([128, 2, 512], F32, st) for _ in range(2)]
            rinv = [C.sb([128, 512], F32, st) for _ in range(2)]
            osb = [C.sb([128, 512], F32, st) for _ in range(2)]
            pst = [C.ps([128, 2, 512], F32, st) for _ in range(2)]
            po = [C.ps([128, 512], F32, st) for _ in range(2)]
            pden = [C.ps([128, 512], F32, st) for _ in range(2)]
            nh = 0; npt = 0; npf = 0
            for qg in range(nqg):
                ob = qg % 2
                qs = slice(qg * 512, (qg + 1) * 512)
                for kp in range(qg + 1):
                    for h2 in range(2):
                        sb_ = nh % 2; nh += 1
                        for cc in range(2):
                            c = h2 * 2 + cc
                            kc = kp * 4 + c
                            n = 2 * kp + h2
                            S.issue('pe', lambda e: e.matmul(pst[sb_][:, cc, :], lhsT=KnT[:, kc * 128:(kc + 1) * 128], rhs=QnT[:, qs], start=True, stop=False),
                                    reads=['KnT', 'QnT'], writes=[('pst', sb_)])
                            S.issue('pe', lambda e: e.matmul(pst[sb_][:, cc, :], lhsT=sel[:, n, :], rhs=mbT[:, qs], start=False, stop=True),
                                    reads=['sel', 'mbT'], writes=[('pst', sb_)])
                        k = npt % 3; npt += 1
                        pat = None
                        if kp == qg: pat = h2 * 2
                        elif kp == qg - 1 and h2 == 1: pat = 4
                        if pat is None:
                            S.issue('act', lambda e: e.activation(out=PT[k][:], in_=pst[sb_][:], func=AF.Exp), reads=[('pst', sb_)], writes=[('PT', k)])
                        else:
                            f = npf % 2; npf += 1
                            if pat == 4: patsl = slice(4, 6)
                            else: patsl = slice(pat, pat + 2)
                            S.issue('act', lambda e: e.activation(out=Pf[f][:], in_=pst[sb_][:], func=AF.Exp), reads=[('pst', sb_)], writes=[('Pf', f)])
                            S.issue('dve', lambda e: e.tensor_tensor(out=PT[k][:], in0=Pf[f][:], in1=ET[:, patsl, :], op=ALU.mult),
                                    reads=[('Pf', f), 'ET'], writes=[('PT', k)])
                        for cc in range(2):
                            kc = kp * 4 + h2 * 2 + cc
                            first = (kp == 0 and h2 == 0 and cc == 0); last = (kp == qg and h2 == 1 and cc == 1)
                            S.issue('pe', lambda e: e.matmul(po[ob][:], lhsT=Vb[:, kc, :], rhs=PT[k][:, cc, :], start=first, stop=last),
                                    reads=['Vb', ('PT', k)], writes=[('po', ob)])
                            S.issue('pe', lambda e: e.matmul(pden[ob][:], lhsT=ones_b[:], rhs=PT[k][:, cc, :], start=first, stop=last),
                                    reads=['ones_b', ('PT', k)], writes=[('pden', ob)])
                S.issue('dve', lambda e: e.reciprocal(out=rinv[ob][:], in_=pden[ob][:]), reads=[('pden', ob)], writes=[('rinv', ob)])
                S.issue('dve', lambda e: e.tensor_tensor(out=osb[ob][:], in0=po[ob][:], in1=rinv[ob][:], op=ALU.mult),
                        reads=[('po', ob), ('rinv', ob)], writes=[('osb', ob)])
                S.issue('dsp', lambda e: e.dma_start(out=oT[rows, qs], in_=osb[ob][:]), reads=[('osb', ob)], writes=[('oT', hh, qg)])
            S.barrier(new_sems=True)
    return C.done()

def _rel_bucket_np(dist):
    n = np.maximum(dist, 0)
    with np.errstate(divide='ignore'):
        large = 16 + (np.log(np.maximum(n, 1).astype(np.float32) / 16) / np.float32(np.log(128 / 16)) * 16).astype(np.int32)
    large = np.minimum(large, 31)
    return np.where(n < 16, n, large)

def moba_consts(rel_bias, heads):
    q = np.arange(512)[None, :]
    kk = np.arange(128)[:, None]
    Tb = np.zeros((128, 2, 6, 512), np.float32)
    for p in range(6):
        koff = (p * 128) if p < 4 else ((p - 4 + 2) * 128 - 512)
        kpos = koff + kk
        dist = q - kpos
        qblk = q // 256; kblk = np.floor_divide(kpos, 256)
        valid = (dist >= 0)
        idx = _rel_bucket_np(dist)
        for hi, h in enumerate(heads):
            Tb[:, hi, p, :] = np.where(valid, rel_bias[idx, h], np.float32(-30000.0))
    return Tb

_PROGS = {}
def _prog(name, fn):
    if name not in _PROGS:
        _PROGS[name] = fn()
    return _PROGS[name]

def _ca(a): return np.ascontiguousarray(a)
def _nwl(v): return _ca(np.asarray(v, np.float32).reshape(-1, 128).T)

def _ffn_launch(hT_shards, lay, ffn_norm, ffn_w_up, ffn_conv_w, ffn_conv_b, ffn_w_down):
    cw = _ca(np.asarray(ffn_conv_w[lay], np.float32).reshape(3, 2 * NFC, 128).transpose(2, 1, 0))
    cb = _ca(np.asarray(ffn_conv_b[lay], np.float32).reshape(2 * NFC, 128).T)
    ims = []
    for c in range(8):
        a = np.zeros((2048, T + 2), np.float32)
        a[:, 2:] = hT_shards[c]
        if c > 0: a[:, :2] = hT_shards[c - 1][:, T - 2:]
        ims.append(dict(aTe=a, nw=_nwl(ffn_norm[lay]), Wup=_ca(np.asarray(ffn_w_up[lay], np.float32)), cw=cw, cb=cb,
                        Wdn=_ca(np.asarray(ffn_w_down[lay], np.float32))))
    r = run(_prog('ffn', build_ffn), ims)
    return [r[c]['hT'] for c in range(8)]

def kernel(x, gla_norm, gla_w_in, gla_gk_w1, gla_gk_w2, gla_gk_b, gla_o_norm, gla_w_out,
           kv_norm, kv_w, k_norm_w, moba_norm, moba_w_q, moba_q_norm, moba_w_out, rel_bias,
           ffn_norm, ffn_w_up, ffn_conv_w, ffn_conv_b, ffn_w_down):
    import ml_dtypes
    f = lambda a: np.asarray(a, np.float32)
    x = f(x)[0]
    xT = [_ca(x[c * T:(c + 1) * T].T) for c in range(8)]
    w2e = _ca(np.concatenate([f(gla_gk_w2)[0], f(gla_gk_b)[0][None]], 0))
    ims = [dict(aT=xT[c], W=_ca(f(gla_w_in)[0]), nw=_nwl(f(gla_norm)[0]), w1=_ca(f(gla_gk_w1)[0]), w2e=w2e) for c in range(8)]
    r1 = run(_prog('glapre', build_glapre), ims)
    qT = np.concatenate([r['qT'] for r in r1], 1); kT = np.concatenate([r['kT'] for r in r1], 1)
    kk = np.concatenate([r['k'] for r in r1], 0); la = np.concatenate([r['la'] for r in r1], 0)
    vv = np.concatenate([r['v'] for r in r1], 0)
    ggT = [r['ggT'] for r in r1]
    del r1
    U = np.triu(np.ones((128, 128), np.float32)); L = np.tril(np.ones((128, 128), np.float32), -1)
    ims = []
    for c in range(8):
        h, j = c // 2, c % 2
        ims.append(dict(qT=_ca(qT[h * 256:(h + 1) * 256]), kT=_ca(kT[h * 256:(h + 1) * 256]), k=_ca(kk[:, h * 256:(h + 1) * 256]),
                        la=_ca(la[:, h * 256:(h + 1) * 256]), v=_ca(vv[:, h * 512 + j * 256:h * 512 + (j + 1) * 256]), U=U, L=L))
    r2 = run(_prog('glarec', build_gla_rec), ims)
    oT = np.concatenate([r['oT'] for r in r2], 0)
    del r2, qT, kT, kk, la, vv
    onw = _ca(np.tile(f(gla_o_norm)[0].reshape(4, 128).T, (1, 4)))
    ims = [dict(aT=_ca(oT[:, c * T:(c + 1) * T]), W=_ca(f(gla_w_out)[0]), nw=onw, gT=ggT[c], resT=xT[c]) for c in range(8)]
    r3 = run(_prog('glapost', build_glapost), ims)
    h1 = [r['hT'] for r in r3]
    del r3, oT, ggT
    h2 = _ffn_launch(h1, 0, ffn_norm, ffn_w_up, ffn_conv_w, ffn_conv_b, ffn_w_down)
    ims = [dict(aT=h2[c], W=_ca(f(kv_w)), nw=_nwl(f(kv_norm))) for c in range(8)]
    r5 = run(_prog('kvproj', build_kvproj), ims)
    KT = np.concatenate([r['KT'] for r in r5], 1); V = np.concatenate([r['V'] for r in r5], 0)
    del r5
    ims = [dict(aT=h2[c], W=_ca(f(moba_w_q)[0]), nw=_nwl(f(moba_norm)[0])) for c in range(8)]
    r5 = run(_prog('qproj', build_qproj), ims)
    QT = np.concatenate([r['QT'] for r in r5], 1)
    del r5
    rb = f(rel_bias)
    ident = np.eye(128, dtype=np.float32)
    sel = np.zeros((128, 64, 128), np.float32)
    for n in range(64): sel[n, n, :] = 1
    sel = sel.astype(ml_dtypes.bfloat16)
    ims = []
    for c in range(8):
        heads = [2 * c, 2 * c + 1]
        ims.append(dict(QT=_ca(QT[c * 256:(c + 1) * 256]), KT=_ca(KT[c * 256:(c + 1) * 256]), V=_ca(V[:, c * 256:(c + 1) * 256]),
                        qw=_ca(np.stack([f(moba_q_norm)[0]] * 2, 1)), kw=_ca(np.stack([f(k_norm_w)] * 2, 1)),
                        c31=_ca(np.broadcast_to(rb[31, heads][None, :], (128, 2))), Tb=moba_consts(rb, heads), ident=ident, sel=sel))
    r6 = run(_prog('moba', build_moba), ims)
    aoT = np.concatenate([r['oT'] for r in r6], 0)
    del r6, QT, KT, V
    ims = [dict(aT=_ca(aoT[:, c * T:(c + 1) * T]), W=_ca(f(moba_w_out)[0]), resT=h2[c]) for c in range(8)]
    r7 = run(_prog('mobaout', build_mobaout), ims)
    h3 = [r['hT'] for r in r7]
    del r7, aoT
    h4 = _ffn_launch(h3, 1, ffn_norm, ffn_w_up, ffn_conv_w, ffn_conv_b, ffn_w_down)
    out = np.concatenate([h.T for h in h4], 0)[None]
    return np.ascontiguousarray(out.astype(np.float32))
```

```python
import contextlib
import numpy as np
import concourse.bass as bass
import concourse.mybir as mybir
from concourse.bass_utils import run_bass_kernel_spmd
F32 = mybir.dt.float32; BF16 = mybir.dt.bfloat16
AF = mybir.ActivationFunctionType; ALU = mybir.AluOpType
EPS = 1e-6

class _Q:
    def __init__(self, name, eng, sem, inc, ename):
        self.name = name; self.eng = eng; self.sem = sem; self.inc = inc; self.count = 0; self.ename = ename

NS_DMA = 8

class Sched:
    def __init__(self, nc, stack):
        self.nc = nc; self.stack = stack; self.epoch = 0
        def sem(n): return stack.enter_context(nc.semaphore(n))
        self.q = {
            'pe': _Q('pe', nc.tensor, sem('s_pe'), 1, 'pe'),
            'act': _Q('act', nc.scalar, sem('s_act'), 1, 'act'),
            'dve': _Q('dve', nc.vector, sem('s_dve'), 1, 'dve'),
            'pool': _Q('pool', nc.gpsimd, sem('s_pool'), 1, 'pool'),
        }
        self.rr = {}
        for qn, eng, en in (('dsp', nc.sync, 'sp'), ('dact', nc.scalar, 'act'), ('dpool', nc.gpsimd, 'pool')):
            self.rr[qn] = 0
            for k in range(NS_DMA):
                self.q[f'{qn}#{k}'] = _Q(f'{qn}#{k}', eng, sem(f's_{qn}_{k}'), 16, en)
        self.engs = {'pe': nc.tensor, 'act': nc.scalar, 'dve': nc.vector, 'pool': nc.gpsimd, 'sp': nc.sync}
        self.lastw = {}; self.reads = {}
        self.seen = {e: {} for e in self.engs}
        self.n_inst = 0; self.n_wait = 0
    def issue(self, qname, fn, reads=(), writes=()):
        if qname in self.rr:
            k = self.rr[qname]; self.rr[qname] = (k + 1) % NS_DMA
            qname = f'{qname}#{k}'
        Qo = self.q[qname]; E = Qo.ename
        deps = {}
        def add(qn, s):
            if qn == qname and qname == 'pe': return
            if deps.get(qn, 0) < s: deps[qn] = s
        for r in reads:
            for qn, s in self.lastw.get(r, {}).items(): add(qn, s)
        for w in writes:
            for qn, s in self.lastw.get(w, {}).items(): add(qn, s)
            for qn, s in self.reads.get(w, {}).items(): add(qn, s)
        seen = self.seen[E]
        for qn, s in deps.items():
            if seen.get(qn, 0) >= s: continue
            dq = self.q[qn]
            Qo.eng.wait_ge(dq.sem, s * dq.inc); self.n_wait += 1
            seen[qn] = s
        ins = fn(Qo.eng)
        Qo.count += 1
        ins.then_inc(Qo.sem, Qo.inc)
        self.n_inst += 1
        seq = Qo.count
        for w in writes:
            self.lastw[w] = {qname: seq}; self.reads[w] = {}
        for r in reads:
            d = self.reads.setdefault(r, {})
            if d.get(qname, 0) < seq: d[qname] = seq
        return ins
    def barrier(self, new_sems=False):
        for en, eng in self.engs.items():
            for qn, dq in self.q.items():
                if dq.count == 0 or self.seen[en].get(qn, 0) >= dq.count: continue
                eng.wait_ge(dq.sem, dq.count * dq.inc); self.seen[en][qn] = dq.count
        self.lastw = {}; self.reads = {}
        if not new_sems: return
        self.epoch += 1
        for qn, dq in self.q.items():
            if dq.count == 0 or dq.inc != 1: continue
            dq.sem = self.stack.enter_context(self.nc.semaphore(f"s_{qn}_{self.epoch}"))
            dq.count = 0
            for en in self.engs: self.seen[en].pop(qn, None)
    def finish(self):
        for qn, dq in self.q.items():
            if dq.count: self.nc.sync.wait_ge(dq.sem, dq.count * dq.inc)

class Ctx:
    def __init__(self, name="k"):
        self.nc = bass.Bass("TRN2", target_bir_lowering=False)
        self.stack = contextlib.ExitStack()
        self.S = Sched(self.nc, self.stack)
        self.uid = 0
    def dram(self, name, shape, dt, kind):
        return self.nc.dram_tensor(name, list(shape), dt, kind=kind).ap()
    def sb(self, shape, dt, stack=None, name=None):
        self.uid += 1
        return (stack or self.stack).enter_context(self.nc.sbuf_tensor(name or f"sb{self.uid}", list(shape), dt))
    def ps(self, shape, dt, stack=None, name=None):
        self.uid += 1
        return (stack or self.stack).enter_context(self.nc.psum_tensor(name or f"ps{self.uid}", list(shape), dt))
    def done(self):
        self.S.finish(); self.stack.close(); return self.nc

def run(nc, in_maps):
    res = run_bass_kernel_spmd(nc, in_maps, core_ids=list(range(8)))
    return res.results

T = 2048
TT = 512

def frontend(C, aT, KC, cols, anT, nw_sb=None, G=None, gT=None, scale_extra=None):
    nc, S = C.nc, C.S
    with contextlib.ExitStack() as st:
        a_sb = C.sb([128, KC, TT], F32, st)
        g_sb = C.sb([128, KC, TT], F32, st) if gT is not None else None
        sq = [C.sb([128, TT], BF16, st) for _ in range(2)]
        ones = C.sb([128, 128], BF16, st)
        ngr = (KC // G) if G else 0
        rt = [C.sb([128, TT], F32, st) for _ in range(max(ngr, 1))]
        rstd = [C.sb([128, TT], F32, st) for _ in range(max(ngr, 1))]
        tmp = [C.sb([128, TT], F32, st) for _ in range(2)]
        pss = C.ps([128, 4, TT], F32, st)
        S.issue('dve', lambda e: e.memset(ones[:], 1.0), writes=['ones'])
        aTv = aT.rearrange("(c p) t -> p c t", p=128)
        gTv = gT.rearrange("(c p) t -> p c t", p=128) if gT is not None else None
        for (c0, w) in cols:
            S.issue('dsp', lambda e: e.dma_start(out=a_sb[:, :, 0:w], in_=aTv[:, :, c0:c0 + w]), writes=['a_sb'])
            if gT is not None:
                S.issue('dact', lambda e: e.dma_start(out=g_sb[:, :, 0:w], in_=gTv[:, :, c0:c0 + w]), writes=['g_sb'])
            if G:
                for c in range(KC):
                    g = c // G
                    S.issue('act', lambda e: e.activation(out=sq[c % 2][:, 0:w], in_=a_sb[:, c, 0:w], func=AF.Square),
                            reads=['a_sb'], writes=[('sq', c % 2)])
                    S.issue('pe', lambda e: e.matmul(pss[:, g, 0:w], lhsT=ones[:], rhs=sq[c % 2][:, 0:w],
                                                     start=(c % G == 0), stop=(c % G == G - 1)),
                            reads=['ones', ('sq', c % 2)], writes=[('pss', g)])
                for g in range(ngr):
                    S.issue('act', lambda e: e.activation(out=rt[g][:, 0:w], in_=pss[:, g, 0:w], func=AF.Sqrt,
                                                          scale=1.0 / (G * 128), bias=EPS),
                            reads=[('pss', g)], writes=[('rt', g)])
                    S.issue('dve', lambda e: e.reciprocal(out=rstd[g][:, 0:w], in_=rt[g][:, 0:w]),
                            reads=[('rt', g)], writes=[('rstd', g)])
            for c in range(KC):
                if G:
                    g = c // G
                    if gT is None:
                        S.issue('dve', lambda e: e.scalar_tensor_tensor(out=anT[:, c, c0:c0 + w], in0=a_sb[:, c, 0:w],
                                scalar=nw_sb[:, c:c + 1], op0=ALU.mult, in1=rstd[g][:, 0:w], op1=ALU.mult),
                                reads=['a_sb', ('rstd', g), 'nw'], writes=[('anT', c, c0)])
                    else:
                        S.issue('dve', lambda e: e.scalar_tensor_tensor(out=tmp[0][:, 0:w], in0=a_sb[:, c, 0:w],
                                scalar=nw_sb[:, c:c + 1], op0=ALU.mult, in1=rstd[g][:, 0:w], op1=ALU.mult),
                                reads=['a_sb', ('rstd', g), 'nw'], writes=[('tmp', 0)])
                        S.issue('act', lambda e: e.activation(out=tmp[1][:, 0:w], in_=g_sb[:, c, 0:w], func=AF.Silu),
                                reads=['g_sb'], writes=[('tmp', 1)])
                        S.issue('dve', lambda e: e.tensor_tensor(out=anT[:, c, c0:c0 + w], in0=tmp[0][:, 0:w],
                                in1=tmp[1][:, 0:w], op=ALU.mult),
                                reads=[('tmp', 0), ('tmp', 1)], writes=[('anT', c, c0)])
                else:
                    q = 'act' if c % 2 else 'dve'
                    if q == 'act':
                        S.issue('act', lambda e: e.activation(out=anT[:, c, c0:c0 + w], in_=a_sb[:, c, 0:w], func=AF.Copy),
                                reads=['a_sb'], writes=[('anT', c, c0)])
                    else:
                        S.issue('dve', lambda e: e.tensor_copy(out=anT[:, c, c0:c0 + w], in_=a_sb[:, c, 0:w]),
                                reads=['a_sb'], writes=[('anT', c, c0)])
        S.barrier()

def build_normproj(KC, N, outs, norm=True, G=None, gate=False, resid=False, gla_gate=False):
    C = Ctx(); nc, S = C.nc, C.S
    aT = C.dram("aT", [KC * 128, T], F32, "ExternalInput")
    W = C.dram("W", [KC * 128, N], F32, "ExternalInput")
    nw = C.dram("nw", [128, KC], F32, "ExternalInput") if norm else None
    gT = C.dram("gT", [KC * 128, T], F32, "ExternalInput") if gate else None
    resT = C.dram("resT", [N, T], F32, "ExternalInput") if resid else None
    od = {}
    for (kind, col0, ncols, dt, name) in outs:
        od[name] = C.dram(name, [ncols, T] if kind == 'f' else [T, ncols], dt, "ExternalOutput")
    if gla_gate:
        w1 = C.dram("w1", [KC * 128, 16], F32, "ExternalInput")
        w2e = C.dram("w2e", [17, 1024], F32, "ExternalInput")
        la = C.dram("la", [T, 1024], F32, "ExternalOutput")
    anT = C.sb([128, KC, T], BF16)
    nw_sb = None
    if norm:
        nw_sb = C.sb([128, KC], F32)
        S.issue('dsp', lambda e: e.dma_start(out=nw_sb[:], in_=nw), writes=['nw'])
    frontend(C, aT, KC, [(i * TT, TT) for i in range(T // TT)], anT, nw_sb, G if norm else None, gT)
    Wv = W.rearrange("(c p) n -> p c n", p=128)
    with contextlib.ExitStack() as st:
        wb = [C.sb([128, KC, 512], BF16, st) for _ in range(2)]
        stf = [C.sb([128, T], F32, st) for _ in range(2)]
        stt = [C.sb([128, 4, 512], F32, st) for _ in range(2)]
        sttb = [C.sb([128, 4, 512], BF16, st) for _ in range(2)]
        r_sb = [C.sb([128, TT], F32, st) for _ in range(4)]
        ps = C.ps([128, 6, 512], F32, st)
        cnt = dict(w=0, b=0, f=0, t=0, r=0, e=0)
        def evac(src, dst, reads, writes):
            cnt['e'] += 1
            if cnt['e'] % 2:
                S.issue('act', lambda e: e.activation(out=dst, in_=src, func=AF.Copy), reads=reads, writes=writes)
            else:
                S.issue('dve', lambda e: e.tensor_copy(out=dst, in_=src), reads=reads, writes=writes)
        if gla_gate:
            w1b = C.sb([128, KC, 16], BF16, st)
            rTe = C.sb([17, T], F32, st)
            w2s = C.sb([17, 1024], F32, st)
            S.issue('dpool', lambda e: e.dma_start(out=w1b[:], in_=w1.rearrange("(c p) n -> p c n", p=128)), writes=['w1b'])
            S.issue('dsp', lambda e: e.dma_start(out=w2s[:], in_=w2e), writes=['w2s'])
            S.issue('dve', lambda e: e.memset(rTe[:], 1.0), writes=['rTe'])
            for tt in range(T // TT):
                b = cnt['b'] % 6; cnt['b'] += 1
                for kc in range(KC):
                    S.issue('pe', lambda e: e.matmul(ps[0:16, b, :], lhsT=w1b[:, kc, :], rhs=anT[:, kc, tt * TT:(tt + 1) * TT],
                                                     start=(kc == 0), stop=(kc == KC - 1)), reads=['w1b'], writes=[('ps', b)])
                S.issue('act', lambda e: e.activation(out=rTe[0:16, tt * TT:(tt + 1) * TT], in_=ps[0:16, b, :], func=AF.Copy),
                        reads=[('ps', b)], writes=['rTe'])
            lav = la.rearrange("(i p) n -> p i n", p=128)
            for ti in range(T // 128):
                sl = (ti // 2) % 2
                for hf in range(2):
                    b = cnt['b'] % 6; cnt['b'] += 1
                    S.issue('pe', lambda e: e.matmul(ps[:, b, :], lhsT=rTe[:, ti * 128:(ti + 1) * 128], rhs=w2s[:, hf * 512:(hf + 1) * 512],
                                                     start=True, stop=True), reads=['rTe', 'w2s'], writes=[('ps', b)])
                    k = cnt['r'] % 4; cnt['r'] += 1
                    S.issue('act', lambda e: e.activation(out=r_sb[k][:], in_=ps[:, b, :], func=AF.Exp, scale=-1.0),
                            reads=[('ps', b)], writes=[('r_sb', k)])
                    S.issue('act', lambda e: e.activation(out=r_sb[k][:], in_=r_sb[k][:], func=AF.Ln, bias=1.0),
                            reads=[('r_sb', k)], writes=[('r_sb', k)])
                    S.issue('dve', lambda e: e.tensor_scalar(out=stt[sl][:, (ti % 2) * 2 + hf, :], in0=r_sb[k][:], scalar1=-1.0 / 16.0,
                                                             scalar2=None, op0=ALU.mult),
                            reads=[('r_sb', k)], writes=[('stt', sl)])
                if ti % 2 == 1:
                    S.issue('dsp', lambda e: e.dma_start(out=lav[:, ti - 1:ti + 1, :], in_=stt[sl][:].rearrange("p (i h) n -> p i (h n)", h=2)),
                            reads=[('stt', sl)], writes=[('la', ti)])
        for (kind, col0, ncols, dt, name) in outs:
            o = od[name]
            for cb in range(0, ncols, 512):
                wcols = min(512, ncols - cb)
                wi = cnt['w'] % 2; cnt['w'] += 1
                S.issue('dpool', lambda e: e.dma_start(out=wb[wi][:, :, 0:wcols], in_=Wv[:, :, col0 + cb:col0 + cb + wcols]),
                        writes=[('wb', wi)])
                if kind == 'f':
                    for j in range(wcols // 128):
                        fi = cnt['f'] % 2; cnt['f'] += 1
                        row0 = cb + j * 128
                        for tt in range(T // TT):
                            b = cnt['b'] % 6; cnt['b'] += 1
                            for kc in range(KC):
                                S.issue('pe', lambda e: e.matmul(ps[:, b, :], lhsT=wb[wi][:, kc, j * 128:(j + 1) * 128],
                                                                 rhs=anT[:, kc, tt * TT:(tt + 1) * TT], start=(kc == 0), stop=(kc == KC - 1)),
                                        reads=[('wb', wi)], writes=[('ps', b)])
                            if resid:
                                k = cnt['r'] % 4; cnt['r'] += 1
                                S.issue('dact', lambda e: e.dma_start(out=r_sb[k][:], in_=resT[col0 + row0:col0 + row0 + 128, tt * TT:(tt + 1) * TT]),
                                        writes=[('r_sb', k)])
                                S.issue('dve', lambda e: e.tensor_tensor(out=stf[fi][:, tt * TT:(tt + 1) * TT], in0=ps[:, b, :], in1=r_sb[k][:], op=ALU.add),
                                        reads=[('ps', b), ('r_sb', k)], writes=[('stf', fi)])
                            else:
                                evac(ps[:, b, :], stf[fi][:, tt * TT:(tt + 1) * TT], [('ps', b)], [('stf', fi)])
                        S.issue('dsp', lambda e: e.dma_start(out=o[row0:row0 + 128, :], in_=stf[fi][:]), reads=[('stf', fi)], writes=[(name, 'f', row0)])
                else:
                    ov = o.rearrange("(i p) n -> p i n", p=128)
                    stg = sttb if dt == BF16 else stt
                    sk = 'sttb' if dt == BF16 else 'stt'
                    for ti in range(T // 128):
                        ti_slot = cnt['t'] % 2
                        b = cnt['b'] % 6; cnt['b'] += 1
                        for kc in range(KC):
                            S.issue('pe', lambda e: e.matmul(ps[:, b, 0:wcols], lhsT=anT[:, kc, ti * 128:(ti + 1) * 128], rhs=wb[wi][:, kc, 0:wcols],
                                                             start=(kc == 0), stop=(kc == KC - 1)), reads=[('wb', wi)], writes=[('ps', b)])
                        evac(ps[:, b, 0:wcols], stg[ti_slot][:, ti % 4, 0:wcols], [('ps', b)], [(sk, ti_slot)])
                        if ti % 4 == 3:
                            S.issue('dsp', lambda e: e.dma_start(out=ov[:, ti - 3:ti + 1, cb:cb + wcols], in_=stg[ti_slot][:, :, 0:wcols]),
                                    reads=[(sk, ti_slot)], writes=[(name, 't', cb, ti)])
                            cnt['t'] += 1
    return C.done()

SEQ = 16384

def build_gla_rec():
    C = Ctx(); nc, S = C.nc, C.S
    qT = C.dram("qT", [256, SEQ], F32, "ExternalInput")
    kT = C.dram("kT", [256, SEQ], F32, "ExternalInput")
    kk = C.dram("k", [SEQ, 256], F32, "ExternalInput")
    la = C.dram("la", [SEQ, 256], F32, "ExternalInput")
    vv = C.dram("v", [SEQ, 256], BF16, "ExternalInput")
    Uc = C.dram("U", [128, 128], F32, "ExternalInput")
    Lc = C.dram("L", [128, 128], F32, "ExternalInput")
    oT = C.dram("oT", [256, SEQ], F32, "ExternalOutput")
    qTv = qT.rearrange("(c p) t -> p c t", p=128); kTv = kT.rearrange("(c p) t -> p c t", p=128)
    kv_ = kk.rearrange("(i p) n -> p i n", p=128); lav = la.rearrange("(i p) n -> p i n", p=128)
    vvv = vv.rearrange("(i p) n -> p i n", p=128); oTv = oT.rearrange("(c p) t -> p c t", p=128)
    U = C.sb([128, 128], F32); L = C.sb([128, 128], F32)
    S.issue('dsp', lambda e: e.dma_start(out=U[:], in_=Uc), writes=['U'])
    S.issue('dsp', lambda e: e.dma_start(out=L[:], in_=Lc), writes=['L'])
    q4 = [C.sb([128, 2, 512], F32) for _ in range(2)]
    k4T = [C.sb([128, 2, 512], F32) for _ in range(2)]
    k4 = [C.sb([128, 4, 256], F32) for _ in range(2)]
    la4 = [C.sb([128, 4, 256], F32) for _ in range(2)]
    v4 = [C.sb([128, 4, 256], BF16) for _ in range(2)]
    ost = [C.sb([128, 2, 512], F32) for _ in range(2)]
    Ep = [C.sb([128, 2, 128], F32) for _ in range(2)]
    Em = [C.sb([128, 2, 128], F32) for _ in range(2)]
    Er = [C.sb([128, 256], F32) for _ in range(2)]
    qd = [C.sb([128, 2, 128], BF16) for _ in range(2)]
    ki = [C.sb([128, 2, 128], BF16) for _ in range(2)]
    ke = [C.sb([128, 256], BF16) for _ in range(2)]
    am = [C.sb([128, 128], BF16) for _ in range(2)]
    St = C.sb([128, 2, 256], F32); Sb = C.sb([128, 2, 256], BF16)
    S.issue('dve', lambda e: e.memset(St[:], 0.0), writes=['St0', 'St1'])
    S.issue('dve', lambda e: e.memset(Sb[:], 0.0), writes=['Sb0', 'Sb1'])
    pb = [C.ps([128, 512], F32) for _ in range(2)]
    prem = C.ps([128, 512], F32); pat = C.ps([128, 512], F32)
    po = [C.ps([128, 512], F32) for _ in range(2)]
    pkv = [C.ps([128, 512], F32) for _ in range(2)]
    for g in range(SEQ // 512):
        gb = g % 2
        ts = slice(g * 512, (g + 1) * 512)
        S.issue('dsp', lambda e: e.dma_start(out=q4[gb][:], in_=qTv[:, :, ts]), writes=[('q4', gb)])
        S.issue('dsp', lambda e: e.dma_start(out=k4T[gb][:], in_=kTv[:, :, ts]), writes=[('k4T', gb)])
        S.issue('dact', lambda e: e.dma_start(out=k4[gb][:], in_=kv_[:, 4 * g:4 * g + 4, :]), writes=[('k4', gb)])
        S.issue('dact', lambda e: e.dma_start(out=la4[gb][:], in_=lav[:, 4 * g:4 * g + 4, :]), writes=[('la4', gb)])
        S.issue('dact', lambda e: e.dma_start(out=v4[gb][:], in_=vvv[:, 4 * g:4 * g + 4, :]), writes=[('v4', gb)])
        for i in range(4):
            n = 4 * g + i; b = n % 2
            cs = slice(i * 128, (i + 1) * 128)
            for kc in range(2):
                S.issue('pe', lambda e: e.matmul(pb[b][:, kc * 128:(kc + 1) * 128], lhsT=la4[gb][:, i, kc * 128:(kc + 1) * 128], rhs=U[:],
                                                 start=True, stop=True), reads=[('la4', gb), 'U'], writes=[('pb', b)])
            S.issue('pe', lambda e: e.matmul(prem[:, 0:256], lhsT=L[:], rhs=la4[gb][:, i, :], start=True, stop=True),
                    reads=[('la4', gb), 'L'], writes=['prem'])
            S.issue('act', lambda e: e.activation(out=Ep[b][:].rearrange("p c t -> p (c t)"), in_=pb[b][:, 0:256], func=AF.Exp),
                    reads=[('pb', b)], writes=[('Ep', b)])
            S.issue('act', lambda e: e.activation(out=Em[b][:].rearrange("p c t -> p (c t)"), in_=pb[b][:, 0:256], func=AF.Exp, scale=-1.0),
                    reads=[('pb', b)], writes=[('Em', b)])
            S.issue('act', lambda e: e.activation(out=Er[b][:], in_=prem[:, 0:256], func=AF.Exp),
                    reads=['prem'], writes=[('Er', b)])
            S.issue('dve', lambda e: e.scalar_tensor_tensor(out=qd[b][:], in0=q4[gb][:, :, cs], scalar=1.0 / 16.0, op0=ALU.mult,
                                                            in1=Ep[b][:], op1=ALU.mult),
                    reads=[('q4', gb), ('Ep', b)], writes=[('qd', b)])
            S.issue('dve', lambda e: e.tensor_tensor(out=ki[b][:], in0=k4T[gb][:, :, cs], in1=Em[b][:], op=ALU.mult),
                    reads=[('k4T', gb), ('Em', b)], writes=[('ki', b)])
            S.issue('pool', lambda e: e.tensor_tensor(out=ke[b][:], in0=k4[gb][:, i, :], in1=Er[b][:], op=ALU.mult),
                    reads=[('k4', gb), ('Er', b)], writes=[('ke', b)])
            for kc in range(2):
                S.issue('pe', lambda e: e.matmul(pat[:, 0:128], lhsT=ki[b][:, kc, :], rhs=qd[b][:, kc, :], start=(kc == 0), stop=(kc == 1)),
                        reads=[('ki', b), ('qd', b)], writes=['pat'])
            S.issue('dve', lambda e: e.tensor_tensor(out=am[b][:], in0=pat[:, 0:128], in1=U[:], op=ALU.mult),
                    reads=['pat', 'U'], writes=[('am', b)])
            for vc in range(2):
                S.issue('pe', lambda e: e.matmul(po[b][:, vc * 128:(vc + 1) * 128], lhsT=v4[gb][:, i, vc * 128:(vc + 1) * 128], rhs=am[b][:],
                                                 start=True, stop=False), reads=[('v4', gb), ('am', b)], writes=[('po', b)])
                for kc in range(2):
                    S.issue('pe', lambda e: e.matmul(po[b][:, vc * 128:(vc + 1) * 128], lhsT=Sb[:, kc, vc * 128:(vc + 1) * 128], rhs=qd[b][:, kc, :],
                                                     start=False, stop=(kc == 1)), reads=[f'Sb{kc}', ('qd', b)], writes=[('po', b)])
            S.issue('act', lambda e: e.activation(out=ost[gb][:, :, cs], in_=po[b][:, 0:256].rearrange("p (c t) -> p c t", c=2), func=AF.Copy),
                    reads=[('po', b)], writes=[('ost', gb)])
            for kc in range(2):
                S.issue('pe', lambda e: e.matmul(pkv[b][:, kc * 256:(kc + 1) * 256], lhsT=ke[b][:, kc * 128:(kc + 1) * 128], rhs=v4[gb][:, i, :],
                                                 start=True, stop=True), reads=[('ke', b), ('v4', gb)], writes=[('pkv', b)])
            for kc in range(2):
                S.issue('dve', lambda e: e.scalar_tensor_tensor(out=St[:, kc, :], in0=St[:, kc, :], scalar=Ep[b][:, kc, 127:128], op0=ALU.mult,
                                                                in1=pkv[b][:, kc * 256:(kc + 1) * 256], op1=ALU.add),
                        reads=[f'St{kc}', ('Ep', b), ('pkv', b)], writes=[f'St{kc}'])
                S.issue('pool', lambda e: e.tensor_copy(out=Sb[:, kc, :], in_=St[:, kc, :]), reads=[f'St{kc}'], writes=[f'Sb{kc}'])
        S.issue('dsp', lambda e: e.dma_start(out=oTv[:, :, ts], in_=ost[gb][:]), reads=[('ost', gb)], writes=[('oT', g)])
    return C.done()

FH = 5632
NFC = FH // 128

def build_ffn():
    C = Ctx(); nc, S = C.nc, C.S
    aTe = C.dram("aTe", [2048, T + 2], F32, "ExternalInput")
    nw = C.dram("nw", [128, 16], F32, "ExternalInput")
    Wup = C.dram("Wup", [2048, 2 * FH], F32, "ExternalInput")
    cwd = C.dram("cw", [128, 2 * NFC, 3], F32, "ExternalInput")
    cbd = C.dram("cb", [128, 2 * NFC], F32, "ExternalInput")
    Wdn = C.dram("Wdn", [FH, 2048], F32, "ExternalInput")
    hT = C.dram("hT", [2048, T], F32, "ExternalOutput")
    actT = C.dram("actT", [FH, T], BF16, "Internal")
    nw_sb = C.sb([128, 16], F32); cw = C.sb([128, 2 * NFC, 3], F32); cb = C.sb([128, 2 * NFC], F32)
    S.issue('dsp', lambda e: e.dma_start(out=nw_sb[:], in_=nw), writes=['nw'])
    S.issue('dsp', lambda e: e.dma_start(out=cw[:], in_=cwd), writes=['cw'])
    S.issue('dsp', lambda e: e.dma_start(out=cb[:], in_=cbd), writes=['cw'])
    Wuv = Wup.rearrange("(c p) n -> p c n", p=128)
    with contextlib.ExitStack() as st1:
        hnT = C.sb([128, 16, T + 2], BF16, st1)
        frontend(C, aTe, 16, [(0, 2)] + [(2 + i * TT, TT) for i in range(T // TT)], hnT, nw_sb, 16)
        wb = [C.sb([128, 16, 256], BF16, st1) for _ in range(2)]
        hbuf = [[C.sb([128, T + 2], F32, st1) for _ in range(2)] for _ in range(2)]
        y = [C.sb([128, T], F32, st1) for _ in range(2)]
        actb = [C.sb([128, T], BF16, st1) for _ in range(2)]
        ps = C.ps([128, 7, 512], F32, st1)
        ph = C.ps([128, 512], F32, st1)
        nb = 0; ne = 0
        for fc in range(NFC):
            wi = fc % 2
            S.issue('dpool', lambda e: e.dma_start(out=wb[wi][:, :, 0:128], in_=Wuv[:, :, fc * 128:(fc + 1) * 128]), writes=[('wb', wi)])
            S.issue('dpool', lambda e: e.dma_start(out=wb[wi][:, :, 128:256], in_=Wuv[:, :, FH + fc * 128:FH + (fc + 1) * 128]), writes=[('wb', wi)])
            for s in range(2):
                hb = hbuf[s][wi]; hk = ('hbuf', s, wi)
                for kc in range(16):
                    S.issue('pe', lambda e: e.matmul(ph[:, s * 8:s * 8 + 2], lhsT=wb[wi][:, kc, s * 128:(s + 1) * 128], rhs=hnT[:, kc, 0:2],
                                                     start=(kc == 0), stop=(kc == 15)), reads=[('wb', wi)], writes=['ph'])
                S.issue('act', lambda e: e.activation(out=hb[:, 0:2], in_=ph[:, s * 8:s * 8 + 2], func=AF.Copy), reads=['ph'], writes=[hk])
                for tt in range(T // TT):
                    b = nb % 7; nb += 1
                    for kc in range(16):
                        S.issue('pe', lambda e: e.matmul(ps[:, b, :], lhsT=wb[wi][:, kc, s * 128:(s + 1) * 128],
                                                         rhs=hnT[:, kc, 2 + tt * TT:2 + (tt + 1) * TT], start=(kc == 0), stop=(kc == 15)),
                                reads=[('wb', wi)], writes=[('ps', b)])
                    ne += 1
                    if ne % 2:
                        S.issue('act', lambda e: e.activation(out=hb[:, 2 + tt * TT:2 + (tt + 1) * TT], in_=ps[:, b, :], func=AF.Copy),
                                reads=[('ps', b)], writes=[hk])
                    else:
                        S.issue('dve', lambda e: e.tensor_copy(out=hb[:, 2 + tt * TT:2 + (tt + 1) * TT], in_=ps[:, b, :]),
                                reads=[('ps', b)], writes=[hk])
                ch = s * NFC + fc
                S.issue('act', lambda e: e.activation(out=y[s][:], in_=hb[:, 2:T + 2], func=AF.Identity, scale=cw[:, ch, 2:3], bias=cb[:, ch:ch + 1]),
                        reads=[hk, 'cw'], writes=[('y', s)])
                S.issue('dve', lambda e: e.scalar_tensor_tensor(out=y[s][:], in0=hb[:, 1:T + 1], scalar=cw[:, ch, 1:2], op0=ALU.mult, in1=y[s][:], op1=ALU.add),
                        reads=[hk, 'cw', ('y', s)], writes=[('y', s)])
                S.issue('dve', lambda e: e.scalar_tensor_tensor(out=y[s][:], in0=hb[:, 0:T], scalar=cw[:, ch, 0:1], op0=ALU.mult, in1=y[s][:], op1=ALU.add),
                        reads=[hk, 'cw', ('y', s)], writes=[('y', s)])
            S.issue('act', lambda e: e.activation(out=y[0][:], in_=y[0][:], func=AF.Silu), reads=[('y', 0)], writes=[('y', 0)])
            S.issue('pool', lambda e: e.tensor_tensor(out=actb[wi][:], in0=y[0][:], in1=y[1][:], op=ALU.mult),
                    reads=[('y', 0), ('y', 1)], writes=[('actb', wi)])
            S.issue('dsp', lambda e: e.dma_start(out=actT[fc * 128:(fc + 1) * 128, :], in_=actb[wi][:]), reads=[('actb', wi)], writes=[('actT', fc)])
        S.barrier()
    actv = actT.rearrange("(c p) t -> p c t", p=128)
    Wdv = Wdn.rearrange("(c p) n -> p c n", p=128)
    with contextlib.ExitStack() as st2:
        act_sb = C.sb([128, NFC, 1024], BF16, st2)
        wd = [C.sb([128, NFC, 256], BF16, st2) for _ in range(2)]
        r_sb = [C.sb([128, TT], F32, st2) for _ in range(4)]
        stg = [C.sb([128, 1024], F32, st2) for _ in range(2)]
        ps = C.ps([128, 6, 512], F32, st2)
        nb = 0; nr = 0; nw_ = 0; ns = 0
        for th in range(2):
            for q4 in range(4):
                S.issue('dsp', lambda e: e.dma_start(out=act_sb[:, q4 * 11:(q4 + 1) * 11, :], in_=actv[:, q4 * 11:(q4 + 1) * 11, th * 1024:(th + 1) * 1024]),
                        reads=[('actT', fcx) for fcx in range(q4 * 11, (q4 + 1) * 11)], writes=['act_sb'])
            for cb_ in range(8):
                wi = nw_ % 2; nw_ += 1
                S.issue('dpool', lambda e: e.dma_start(out=wd[wi][:], in_=Wdv[:, :, cb_ * 256:(cb_ + 1) * 256]), writes=[('wd', wi)])
                for j in range(2):
                    si = ns % 2; ns += 1
                    row0 = cb_ * 256 + j * 128
                    for t2 in range(2):
                        b = nb % 6; nb += 1
                        for fcc in range(NFC):
                            S.issue('pe', lambda e: e.matmul(ps[:, b, :], lhsT=wd[wi][:, fcc, j * 128:(j + 1) * 128], rhs=act_sb[:, fcc, t2 * 512:(t2 + 1) * 512],
                                                             start=(fcc == 0), stop=(fcc == NFC - 1)), reads=[('wd', wi), 'act_sb'], writes=[('ps', b)])
                        k = nr % 4; nr += 1
                        c0 = 2 + th * 1024 + t2 * 512
                        S.issue('dact', lambda e: e.dma_start(out=r_sb[k][:], in_=aTe[row0:row0 + 128, c0:c0 + 512]), writes=[('r_sb', k)])
                        S.issue('dve', lambda e: e.tensor_tensor(out=stg[si][:, t2 * 512:(t2 + 1) * 512], in0=ps[:, b, :], in1=r_sb[k][:], op=ALU.add),
                                reads=[('ps', b), ('r_sb', k)], writes=[('stg', si)])
                    S.issue('dsp', lambda e: e.dma_start(out=hT[row0:row0 + 128, th * 1024:(th + 1) * 1024], in_=stg[si][:]), reads=[('stg', si)], writes=[('hT', th, row0)])
    return C.done()

def build_glapre():
    return build_normproj(16, 6144, [('f', 0, 1024, F32, 'qT'), ('f', 1024, 1024, F32, 'kT'), ('f', 4096, 2048, F32, 'ggT'),
                                     ('t', 1024, 1024, F32, 'k'), ('t', 2048, 2048, BF16, 'v')], norm=True, G=16, gla_gate=True)
def build_glapost():
    return build_normproj(16, 2048, [('f', 0, 2048, F32, 'hT')], norm=True, G=4, gate=True, resid=True)
def build_kvproj():
    return build_normproj(16, 4096, [('f', 0, 2048, F32, 'KT'), ('t', 2048, 2048, BF16, 'V')], norm=True, G=16)
def build_qproj():
    return build_normproj(16, 2048, [('f', 0, 2048, F32, 'QT')], norm=True, G=16)
def build_mobaout():
    return build_normproj(16, 2048, [('f', 0, 2048, F32, 'hT')], norm=False, resid=True)

NBLK = 64

def build_moba(nheads=2, nqg=SEQ // 512, ntiles=SEQ // 512, do_gate=True, skip=()):
    C = Ctx(); nc, S = C.nc, C.S
    QT = C.dram("QT", [256, SEQ], F32, "ExternalInput")
    KT = C.dram("KT", [256, SEQ], F32, "ExternalInput")
    V = C.dram("V", [SEQ, 256], BF16, "ExternalInput")
    qwd = C.dram("qw", [128, 2], F32, "ExternalInput")
    kwd = C.dram("kw", [128, 2], F32, "ExternalInput")
    c31d = C.dram("c31", [128, 2], F32, "ExternalInput")
    Tbd = C.dram("Tb", [128, 2, 6, 512], F32, "ExternalInput")
    identd = C.dram("ident", [128, 128], F32, "ExternalInput")
    seld = C.dram("sel", [128, 64, 128], BF16, "ExternalInput")
    oT = C.dram("oT", [256, SEQ], F32, "ExternalOutput")
    Vv = V.rearrange("(i p) n -> p i n", p=128)
    qw = C.sb([128, 2], F32); kw = C.sb([128, 2], F32); c31 = C.sb([128, 2], F32); nc31 = C.sb([128, 2], F32)
    ident = C.sb([128, 128], F32); sel = C.sb([128, 64, 128], BF16)
    ones_b = C.sb([128, 128], BF16)
    for dst, src, k in ((qw, qwd, 'qw'), (kw, kwd, 'kw'), (c31, c31d, 'c31'), (ident, identd, 'ident'), (sel, seld, 'sel')):
        S.issue('dsp', lambda e: e.dma_start(out=dst[:], in_=src), writes=[k])
    S.issue('dve', lambda e: e.memset(ones_b[:], 1.0), writes=['ones_b'])
    S.issue('dve', lambda e: e.tensor_scalar(out=nc31[:], in0=c31[:], scalar1=-1.0, scalar2=None, op0=ALU.mult), reads=['c31'], writes=['nc31'])
    QnT = C.sb([128, SEQ], BF16); KnT = C.sb([128, SEQ], BF16); Vb = C.sb([128, 128, 128], BF16)
    mbT = C.sb([128, SEQ], BF16); ET = C.sb([128, 6, 512], F32)
    S.issue('pool', lambda e: e.memset(mbT[64:128, :], 0.0), writes=['mbT'])
    for hh in range(nheads):
        rows = slice(hh * 128, (hh + 1) * 128)
        with contextlib.ExitStack() as st:
            xin = [C.sb([128, 512], F32, st) for _ in range(2)]
            sq = [C.sb([128, 512], BF16, st) for _ in range(2)]
            rt = [C.sb([128, 512], F32, st) for _ in range(2)]
            nf = [C.sb([128, 512], F32, st) for _ in range(2)]
            kmean = C.sb([128, NBLK], F32, st)
            gs = [C.sb([128, NBLK], F32, st) for _ in range(2)]
            m8 = [C.sb([128, 8], F32, st) for _ in range(2)]
            mb = [C.sb([128, 4, NBLK], F32, st) for _ in range(2)]
            pss = C.ps([128, 2, 512], F32, st)
            pg = C.ps([128, 4, 128], F32, st)
            pmt = C.ps([128, 2, 512], F32, st)
            S.issue('dsp', lambda e: e.dma_start(out=ET[:], in_=Tbd[:, hh, :, :]), writes=['ET'])
            if 'et' not in skip:
                S.issue('act', lambda e: e.activation(out=ET[:], in_=ET[:], func=AF.Exp, bias=nc31[:, hh:hh + 1]), reads=['ET', 'nc31'], writes=['ET'])
            for half in range(0 if 'vb' in skip else 8):
                S.issue('dact', lambda e: e.dma_start(out=Vb[:, half * 16:(half + 1) * 16, :], in_=Vv[:, half * 16:(half + 1) * 16, rows]), writes=['Vb'])
            for which in range(2):
                src = KT if which == 0 else QT
                wv = kw if which == 0 else qw
                for t in range(ntiles):
                    b = t % 2
                    ts = slice(t * 512, (t + 1) * 512)
                    S.issue('dsp', lambda e: e.dma_start(out=xin[b][:], in_=src[rows, ts]), writes=[('xin', b)])
                    S.issue('act', lambda e: e.activation(out=sq[b][:], in_=xin[b][:], func=AF.Square), reads=[('xin', b)], writes=[('sq', b)])
                    S.issue('pe', lambda e: e.matmul(pss[:, b, :], lhsT=ones_b[:], rhs=sq[b][:], start=True, stop=True),
                            reads=['ones_b', ('sq', b)], writes=[('pss', b)])
                    S.issue('act', lambda e: e.activation(out=rt[b][:], in_=pss[:, b, :], func=AF.Sqrt, scale=1.0 / 128, bias=EPS),
                            reads=[('pss', b)], writes=[('rt', b)])
                    S.issue('dve', lambda e: e.reciprocal(out=rt[b][:], in_=rt[b][:]), reads=[('rt', b)], writes=[('rt', b)])
                    S.issue('dve', lambda e: e.scalar_tensor_tensor(out=nf[b][:], in0=xin[b][:], scalar=wv[:, hh:hh + 1], op0=ALU.mult,
                                                                    in1=rt[b][:], op1=ALU.mult),
                            reads=[('xin', b), ('rt', b), 'qw', 'kw'], writes=[('nf', b)])
                    if which == 0:
                        S.issue('act', lambda e: e.activation(out=KnT[:, ts], in_=nf[b][:], func=AF.Copy), reads=[('nf', b)], writes=['KnT'])
                        if 'km' not in skip: S.issue('dve', lambda e: e.tensor_reduce(out=kmean[:, 2 * t:2 * t + 2], in_=nf[b][:].rearrange("p (a c) -> p a c", a=2),
                                                                 axis=mybir.AxisListType.X, op=ALU.add), reads=[('nf', b)], writes=['kmean'])
                    else:
                        S.issue('act', lambda e: e.activation(out=QnT[:, ts], in_=nf[b][:], func=AF.Copy, scale=128.0 ** -0.5),
                                reads=[('nf', b)], writes=['QnT'])
                        for i in range(4):
                            qt = 4 * t + i; own = qt // 2
                            if own >= 3 and do_gate:
                                S.issue('pe', lambda e: e.matmul(pg[:, i, 0:NBLK], lhsT=nf[b][:, i * 128:(i + 1) * 128], rhs=kmean[:], start=True, stop=True),
                                        reads=[('nf', b), 'kmean'], writes=['pg'])
                                g2 = i % 2
                                S.issue('pool', lambda e: e.memset(gs[g2][:], -1e30), writes=[('gs', g2)])
                                S.issue('dve', lambda e: e.tensor_copy(out=gs[g2][:, 0:own], in_=pg[:, i, 0:own]), reads=['pg'], writes=[('gs', g2)])
                                if 'mx' not in skip: S.issue('dve', lambda e: e.max(out=m8[g2][:], in_=gs[g2][:]), reads=[('gs', g2)], writes=[('m8', g2)])
                                else: S.issue('dve', lambda e: e.tensor_copy(out=m8[g2][:], in_=gs[g2][:, 0:8]), reads=[('gs', g2)], writes=[('m8', g2)])
                                if 'ts' not in skip: S.issue('dve', lambda e: e.tensor_scalar(out=gs[g2][:], in0=gs[g2][:], scalar1=m8[g2][:, 2:3], op0=ALU.is_ge,
                                                                         scalar2=30000.0, op1=ALU.mult), reads=[('gs', g2), ('m8', g2)], writes=[('gs', g2)])
                                S.issue('dve', lambda e: e.tensor_scalar(out=mb[b][:, i, :], in0=gs[g2][:], scalar1=-30000.0, scalar2=None, op0=ALU.add),
                                        reads=[('gs', g2)], writes=[('mb', b)])
                                S.issue('dve', lambda e: e.memset(mb[b][:, i, own:own + 1], 0.0), writes=[('mb', b)])
                            else:
                                S.issue('dve', lambda e: e.memset(mb[b][:, i, :], 0.0), writes=[('mb', b)])
                        for i in range(0 if 'tr' in skip else 4):
                            S.issue('pe', lambda e: e.transpose(pmt[0:64, b, i * 128:(i + 1) * 128], mb[b][:, i, :], ident[:]),
                                    reads=[('mb', b), 'ident'], writes=[('pmt', b)])
                        if 'tr' not in skip: S.issue('act', lambda e: e.activation(out=mbT[0:64, ts], in_=pmt[0:64, b, :], func=AF.Copy), reads=[('pmt', b)], writes=['mbT'])
                if which == 0:
                    S.issue('dve', lambda e: e.tensor_scalar(out=kmean[:], in0=kmean[:], scalar1=1.0 / 256, scalar2=None, op0=ALU.mult),
                            reads=['kmean'], writes=['kmean'])
            S.barrier(new_sems=True)
        with contextlib.ExitStack() as st:
            NPT = 4
            PT = [C.sb([128, 2, 512], BF16, st) for _ in range(NPT)]
            Pf = [C.sb([128, 2, 512], F32, st) for _ in range(2)]
            rinv = [C.sb([128, 512], F32, st) for _ in range(2)]
            osb = [C.sb([128, 512], F32, st) for _ in range(2)]
            pst = [C.ps([128, 2, 512], F32, st) for _ in range(2)]
            po = [C.ps([128, 512], F32, st) for _ in range(2)]
            pden = [C.ps([128, 512], F32, st) for _ in range(2)]
            steps = [(qg, kp, h2) for qg in range(nqg) for kp in range(qg + 1) for h2 in range(2)]
            npf = [0]
            def emit_qk(idx):
                qg, kp, h2 = steps[idx]
                sb_ = idx % 2; k = idx % NPT
                qs = slice(qg * 512, (qg + 1) * 512)
                n = 2 * kp + h2
                for cc in range(2):
                    kc = kp * 4 + h2 * 2 + cc
                    S.issue('pe', lambda e: e.matmul(pst[sb_][:, cc, :], lhsT=KnT[:, kc * 128:(kc + 1) * 128], rhs=QnT[:, qs], start=True, stop=False),
                            reads=['KnT', 'QnT'], writes=[('pst', sb_)])
                    S.issue('pe', lambda e: e.matmul(pst[sb_][:, cc, :], lhsT=sel[:, n, :], rhs=mbT[:, qs], start=False, stop=True),
                            reads=['sel', 'mbT'], writes=[('pst', sb_)])
                pat = None
                if kp == qg: pat = h2 * 2
                elif kp == qg - 1 and h2 == 1: pat = 4
                if pat is None:
                    S.issue('act', lambda e: e.activation(out=PT[k][:], in_=pst[sb_][:], func=AF.Exp), reads=[('pst', sb_)], writes=[('PT', k)])
                else:
                    f = npf[0] % 2; npf[0] += 1
                    patsl = slice(4, 6) if pat == 4 else slice(pat, pat + 2)
                    S.issue('act', lambda e: e.activation(out=Pf[f][:], in_=pst[sb_][:], func=AF.Exp), reads=[('pst', sb_)], writes=[('Pf', f)])
                    S.issue('dve', lambda e: e.tensor_tensor(out=PT[k][:], in0=Pf[f][:], in1=ET[:, patsl, :], op=ALU.mult),
                            reads=[('Pf', f), 'ET'], writes=[('PT', k)])
            def emit_pv(idx):
                qg, kp, h2 = steps[idx]
                k = idx % NPT; ob = qg % 2
                qs = slice(qg * 512, (qg + 1) * 512)
                for cc in range(2):
                    kc = kp * 4 + h2 * 2 + cc
                    first = (kp == 0 and h2 == 0 and cc == 0); last = (kp == qg and h2 == 1 and cc == 1)
                    S.issue('pe', lambda e: e.matmul(po[ob][:], lhsT=Vb[:, kc, :], rhs=PT[k][:, cc, :], start=first, stop=last),
                            reads=['Vb', ('PT', k)], writes=[('po', ob)])
                    S.issue('pe', lambda e: e.matmul(pden[ob][:], lhsT=ones_b[:], rhs=PT[k][:, cc, :], start=first, stop=last),
                            reads=['ones_b', ('PT', k)], writes=[('pden', ob)])
                if kp == qg and h2 == 1:
                    S.issue('dve', lambda e: e.reciprocal(out=rinv[ob][:], in_=pden[ob][:]), reads=[('pden', ob)], writes=[('rinv', ob)])
                    S.issue('dve', lambda e: e.tensor_tensor(out=osb[ob][:], in0=po[ob][:], in1=rinv[ob][:], op=ALU.mult),
                            reads=[('po', ob), ('rinv', ob)], writes=[('osb', ob)])
                    S.issue('dsp', lambda e: e.dma_start(out=oT[rows, qs], in_=osb[ob][:]), reads=[('osb', ob)], writes=[('oT', hh, qg)])
            for idx in range(len(steps)):
                emit_qk(idx)
                if idx >= 1: emit_pv(idx - 1)
            if steps: emit_pv(len(steps) - 1)
            S.barrier(new_sems=True)
    return C.done()

def _rel_bucket_np(dist):
    n = np.maximum(dist, 0)
    with np.errstate(divide='ignore'):
        large = 16 + (np.log(np.maximum(n, 1).astype(np.float32) / 16) / np.float32(np.log(128 / 16)) * 16).astype(np.int32)
    large = np.minimum(large, 31)
    return np.where(n < 16, n, large)

def moba_consts(rel_bias, heads):
    q = np.arange(512)[None, :]
    kk = np.arange(128)[:, None]
    Tb = np.zeros((128, 2, 6, 512), np.float32)
    for p in range(6):
        koff = (p * 128) if p < 4 else ((p - 4 + 2) * 128 - 512)
        kpos = koff + kk
        dist = q - kpos
        qblk = q // 256; kblk = np.floor_divide(kpos, 256)
        valid = (dist >= 0)
        idx = _rel_bucket_np(dist)
        for hi, h in enumerate(heads):
            Tb[:, hi, p, :] = np.where(valid, rel_bias[idx, h], np.float32(-30000.0))
    return Tb

_PROGS = {}
def _prog(name, fn):
    if name not in _PROGS:
        _PROGS[name] = fn()
    return _PROGS[name]

def _ca(a): return np.ascontiguousarray(a)
def _nwl(v): return _ca(np.asarray(v, np.float32).reshape(-1, 128).T)

def _ffn_launch(hT_shards, lay, ffn_norm, ffn_w_up, ffn_conv_w, ffn_conv_b, ffn_w_down):
    cw = _ca(np.asarray(ffn_conv_w[lay], np.float32).reshape(3, 2 * NFC, 128).transpose(2, 1, 0))
    cb = _ca(np.asarray(ffn_conv_b[lay], np.float32).reshape(2 * NFC, 128).T)
    ims = []
    for c in range(8):
        a = np.zeros((2048, T + 2), np.float32)
        a[:, 2:] = hT_shards[c]
        if c > 0: a[:, :2] = hT_shards[c - 1][:, T - 2:]
        ims.append(dict(aTe=a, nw=_nwl(ffn_norm[lay]), Wup=_ca(np.asarray(ffn_w_up[lay], np.float32)), cw=cw, cb=cb,
                        Wdn=_ca(np.asarray(ffn_w_down[lay], np.float32))))
    r = run(_prog('ffn', build_ffn), ims)
    return [r[c]['hT'] for c in range(8)]

def kernel(x, gla_norm, gla_w_in, gla_gk_w1, gla_gk_w2, gla_gk_b, gla_o_norm, gla_w_out,
           kv_norm, kv_w, k_norm_w, moba_norm, moba_w_q, moba_q_norm, moba_w_out, rel_bias,
           ffn_norm, ffn_w_up, ffn_conv_w, ffn_conv_b, ffn_w_down):
    import ml_dtypes
    f = lambda a: np.asarray(a, np.float32)
    x = f(x)[0]
    xT = [_ca(x[c * T:(c + 1) * T].T) for c in range(8)]
    w2e = _ca(np.concatenate([f(gla_gk_w2)[0], f(gla_gk_b)[0][None]], 0))
    ims = [dict(aT=xT[c], W=_ca(f(gla_w_in)[0]), nw=_nwl(f(gla_norm)[0]), w1=_ca(f(gla_gk_w1)[0]), w2e=w2e) for c in range(8)]
    r1 = run(_prog('glapre', build_glapre), ims)
    qT = np.concatenate([r['qT'] for r in r1], 1); kT = np.concatenate([r['kT'] for r in r1], 1)
    kk = np.concatenate([r['k'] for r in r1], 0); la = np.concatenate([r['la'] for r in r1], 0)
    vv = np.concatenate([r['v'] for r in r1], 0)
    ggT = [r['ggT'] for r in r1]
    del r1
    U = np.triu(np.ones((128, 128), np.float32)); L = np.tril(np.ones((128, 128), np.float32), -1)
    ims = []
    for c in range(8):
        h, j = c // 2, c % 2
        ims.append(dict(qT=_ca(qT[h * 256:(h + 1) * 256]), kT=_ca(kT[h * 256:(h + 1) * 256]), k=_ca(kk[:, h * 256:(h + 1) * 256]),
                        la=_ca(la[:, h * 256:(h + 1) * 256]), v=_ca(vv[:, h * 512 + j * 256:h * 512 + (j + 1) * 256]), U=U, L=L))
    r2 = run(_prog('glarec', build_gla_rec), ims)
    oT = np.concatenate([r['oT'] for r in r2], 0)
    del r2, qT, kT, kk, la, vv
    onw = _ca(np.tile(f(gla_o_norm)[0].reshape(4, 128).T, (1, 4)))
    ims = [dict(aT=_ca(oT[:, c * T:(c + 1) * T]), W=_ca(f(gla_w_out)[0]), nw=onw, gT=ggT[c], resT=xT[c]) for c in range(8)]
    r3 = run(_prog('glapost', build_glapost), ims)
    h1 = [r['hT'] for r in r3]
    del r3, oT, ggT
    h2 = _ffn_launch(h1, 0, ffn_norm, ffn_w_up, ffn_conv_w, ffn_conv_b, ffn_w_down)
    ims = [dict(aT=h2[c], W=_ca(f(kv_w)), nw=_nwl(f(kv_norm))) for c in range(8)]
    r5 = run(_prog('kvproj', build_kvproj), ims)
    KT = np.concatenate([r['KT'] for r in r5], 1); V = np.concatenate([r['V'] for r in r5], 0)
    del r5
    ims = [dict(aT=h2[c], W=_ca(f(moba_w_q)[0]), nw=_nwl(f(moba_norm)[0])) for c in range(8)]
    r5 = run(_prog('qproj', build_qproj), ims)
    QT = np.concatenate([r['QT'] for r in r5], 1)
    del r5
    rb = f(rel_bias)
    ident = np.eye(128, dtype=np.float32)
    sel = np.zeros((128, 64, 128), np.float32)
    for n in range(64): sel[n, n, :] = 1
    sel = sel.astype(ml_dtypes.bfloat16)
    ims = []
    for c in range(8):
        heads = [2 * c, 2 * c + 1]
        ims.append(dict(QT=_ca(QT[c * 256:(c + 1) * 256]), KT=_ca(KT[c * 256:(c + 1) * 256]), V=_ca(V[:, c * 256:(c + 1) * 256]),
                        qw=_ca(np.stack([f(moba_q_norm)[0]] * 2, 1)), kw=_ca(np.stack([f(k_norm_w)] * 2, 1)),
                        c31=_ca(np.broadcast_to(rb[31, heads][None, :], (128, 2))), Tb=moba_consts(rb, heads), ident=ident, sel=sel))
    r6 = run(_prog('moba', build_moba), ims)
    aoT = np.concatenate([r['oT'] for r in r6], 0)
    del r6, QT, KT, V
    ims = [dict(aT=_ca(aoT[:, c * T:(c + 1) * T]), W=_ca(f(moba_w_out)[0]), resT=h2[c]) for c in range(8)]
    r7 = run(_prog('mobaout', build_mobaout), ims)
    h3 = [r['hT'] for r in r7]
    del r7, aoT
    h4 = _ffn_launch(h3, 1, ffn_norm, ffn_w_up, ffn_conv_w, ffn_conv_b, ffn_w_down)
    out = np.concatenate([h.T for h in h4], 0)[None]
    return np.ascontiguousarray(out.astype(np.float32))
```

```python
import contextlib
import numpy as np
import concourse.bass as bass
import concourse.mybir as mybir
from concourse.bass_utils import run_bass_kernel_spmd
F32 = mybir.dt.float32; BF16 = mybir.dt.bfloat16
AF = mybir.ActivationFunctionType; ALU = mybir.AluOpType
EPS = 1e-6

class _Q:
    def __init__(self, name, eng, sem, inc, ename):
        self.name = name; self.eng = eng; self.sem = sem; self.inc = inc; self.count = 0; self.ename = ename

NS_DMA = 8

class Sched:
    def __init__(self, nc, stack):
        self.nc = nc; self.stack = stack; self.epoch = 0
        def sem(n): return stack.enter_context(nc.semaphore(n))
        self.q = {
            'pe': _Q('pe', nc.tensor, sem('s_pe'), 1, 'pe'),
            'act': _Q('act', nc.scalar, sem('s_act'), 1, 'act'),
            'dve': _Q('dve', nc.vector, sem('s_dve'), 1, 'dve'),
            'pool': _Q('pool', nc.gpsimd, sem('s_pool'), 1, 'pool'),
        }
        self.rr = {}
        for qn, eng, en in (('dsp', nc.sync, 'sp'), ('dact', nc.scalar, 'act'), ('dpool', nc.gpsimd, 'pool')):
            self.rr[qn] = 0
            for k in range(NS_DMA):
                self.q[f'{qn}#{k}'] = _Q(f'{qn}#{k}', eng, sem(f's_{qn}_{k}'), 16, en)
        self.engs = {'pe': nc.tensor, 'act': nc.scalar, 'dve': nc.vector, 'pool': nc.gpsimd, 'sp': nc.sync}
        self.lastw = {}; self.reads = {}
        self.seen = {e: {} for e in self.engs}
        self.n_inst = 0; self.n_wait = 0
    def issue(self, qname, fn, reads=(), writes=()):
        if qname in self.rr:
            k = self.rr[qname]; self.rr[qname] = (k + 1) % NS_DMA
            qname = f'{qname}#{k}'
        Qo = self.q[qname]; E = Qo.ename
        deps = {}
        def add(qn, s):
            if qn == qname and qname == 'pe': return
            if deps.get(qn, 0) < s: deps[qn] = s
        for r in reads:
            for qn, s in self.lastw.get(r, {}).items(): add(qn, s)
        for w in writes:
            for qn, s in self.lastw.get(w, {}).items(): add(qn, s)
            for qn, s in self.reads.get(w, {}).items(): add(qn, s)
        seen = self.seen[E]
        for qn, s in deps.items():
            if seen.get(qn, 0) >= s: continue
            dq = self.q[qn]
            Qo.eng.wait_ge(dq.sem, s * dq.inc); self.n_wait += 1
            seen[qn] = s
        ins = fn(Qo.eng)
        Qo.count += 1
        ins.then_inc(Qo.sem, Qo.inc)
        self.n_inst += 1
        seq = Qo.count
        for w in writes:
            self.lastw[w] = {qname: seq}; self.reads[w] = {}
        for r in reads:
            d = self.reads.setdefault(r, {})
            if d.get(qname, 0) < seq: d[qname] = seq
        return ins
    def barrier(self, new_sems=False):
        for en, eng in self.engs.items():
            for qn, dq in self.q.items():
                if dq.count == 0 or self.seen[en].get(qn, 0) >= dq.count: continue
                eng.wait_ge(dq.sem, dq.count * dq.inc); self.seen[en][qn] = dq.count
        self.lastw = {}; self.reads = {}
        if not new_sems: return
        self.epoch += 1
        for qn, dq in self.q.items():
            if dq.count == 0 or dq.inc != 1: continue
            dq.sem = self.stack.enter_context(self.nc.semaphore(f"s_{qn}_{self.epoch}"))
            dq.count = 0
            for en in self.engs: self.seen[en].pop(qn, None)
    def finish(self):
        for qn, dq in self.q.items():
            if dq.count: self.nc.sync.wait_ge(dq.sem, dq.count * dq.inc)

class Ctx:
    def __init__(self, name="k"):
        self.nc = bass.Bass("TRN2", target_bir_lowering=False)
        self.stack = contextlib.ExitStack()
        self.S = Sched(self.nc, self.stack)
        self.uid = 0
    def dram(self, name, shape, dt, kind):
        return self.nc.dram_tensor(name, list(shape), dt, kind=kind).ap()
    def sb(self, shape, dt, stack=None, name=None):
        self.uid += 1
        return (stack or self.stack).enter_context(self.nc.sbuf_tensor(name or f"sb{self.uid}", list(shape), dt))
    def ps(self, shape, dt, stack=None, name=None):
        self.uid += 1
        return (stack or self.stack).enter_context(self.nc.psum_tensor(name or f"ps{self.uid}", list(shape), dt))
    def done(self):
        self.S.finish(); self.stack.close(); return self.nc

def run(nc, in_maps):
    res = run_bass_kernel_spmd(nc, in_maps, core_ids=list(range(8)))
    return res.results

T = 2048
TT = 512

def frontend(C, aT, KC, cols, anT, nw_sb=None, G=None, gT=None, scale_extra=None):
    nc, S = C.nc, C.S
    with contextlib.ExitStack() as st:
        a_sb = C.sb([128, KC, TT], F32, st)
        g_sb = C.sb([128, KC, TT], F32, st) if gT is not None else None
        sq = [C.sb([128, TT], BF16, st) for _ in range(2)]
        ones = C.sb([128, 128], BF16, st)
        ngr = (KC // G) if G else 0
        rt = [C.sb([128, TT], F32, st) for _ in range(max(ngr, 1))]
        rstd = [C.sb([128, TT], F32, st) for _ in range(max(ngr, 1))]
        tmp = [C.sb([128, TT], F32, st) for _ in range(2)]
        pss = C.ps([128, 4, TT], F32, st)
        S.issue('dve', lambda e: e.memset(ones[:], 1.0), writes=['ones'])
        aTv = aT.rearrange("(c p) t -> p c t", p=128)
        gTv = gT.rearrange("(c p) t -> p c t", p=128) if gT is not None else None
        for (c0, w) in cols:
            S.issue('dsp', lambda e: e.dma_start(out=a_sb[:, :, 0:w], in_=aTv[:, :, c0:c0 + w]), writes=['a_sb'])
            if gT is not None:
                S.issue('dact', lambda e: e.dma_start(out=g_sb[:, :, 0:w], in_=gTv[:, :, c0:c0 + w]), writes=['g_sb'])
            if G:
                for c in range(KC):
                    g = c // G
                    S.issue('act', lambda e: e.activation(out=sq[c % 2][:, 0:w], in_=a_sb[:, c, 0:w], func=AF.Square),
                            reads=['a_sb'], writes=[('sq', c % 2)])
                    S.issue('pe', lambda e: e.matmul(pss[:, g, 0:w], lhsT=ones[:], rhs=sq[c % 2][:, 0:w],
                                                     start=(c % G == 0), stop=(c % G == G - 1)),
                            reads=['ones', ('sq', c % 2)], writes=[('pss', g)])
                for g in range(ngr):
                    S.issue('act', lambda e: e.activation(out=rt[g][:, 0:w], in_=pss[:, g, 0:w], func=AF.Sqrt,
                                                          scale=1.0 / (G * 128), bias=EPS),
                            reads=[('pss', g)], writes=[('rt', g)])
                    S.issue('dve', lambda e: e.reciprocal(out=rstd[g][:, 0:w], in_=rt[g][:, 0:w]),
                            reads=[('rt', g)], writes=[('rstd', g)])
            for c in range(KC):
                if G:
                    g = c // G
                    if gT is None:
                        S.issue('dve', lambda e: e.scalar_tensor_tensor(out=anT[:, c, c0:c0 + w], in0=a_sb[:, c, 0:w],
                                scalar=nw_sb[:, c:c + 1], op0=ALU.mult, in1=rstd[g][:, 0:w], op1=ALU.mult),
                                reads=['a_sb', ('rstd', g), 'nw'], writes=[('anT', c, c0)])
                    else:
                        S.issue('dve', lambda e: e.scalar_tensor_tensor(out=tmp[0][:, 0:w], in0=a_sb[:, c, 0:w],
                                scalar=nw_sb[:, c:c + 1], op0=ALU.mult, in1=rstd[g][:, 0:w], op1=ALU.mult),
                                reads=['a_sb', ('rstd', g), 'nw'], writes=[('tmp', 0)])
                        S.issue('act', lambda e: e.activation(out=tmp[1][:, 0:w], in_=g_sb[:, c, 0:w], func=AF.Silu),
                                reads=['g_sb'], writes=[('tmp', 1)])
                        S.issue('dve', lambda e: e.tensor_tensor(out=anT[:, c, c0:c0 + w], in0=tmp[0][:, 0:w],
                                in1=tmp[1][:, 0:w], op=ALU.mult),
                                reads=[('tmp', 0), ('tmp', 1)], writes=[('anT', c, c0)])
                else:
                    q = 'act' if c % 2 else 'dve'
                    if q == 'act':
                        S.issue('act', lambda e: e.activation(out=anT[:, c, c0:c0 + w], in_=a_sb[:, c, 0:w], func=AF.Copy),
                                reads=['a_sb'], writes=[('anT', c, c0)])
                    else:
                        S.issue('dve', lambda e: e.tensor_copy(out=anT[:, c, c0:c0 + w], in_=a_sb[:, c, 0:w]),
                                reads=['a_sb'], writes=[('anT', c, c0)])
        S.barrier()

def build_normproj(KC, N, outs, norm=True, G=None, gate=False, resid=False, gla_gate=False):
    C = Ctx(); nc, S = C.nc, C.S
    aT = C.dram("aT", [KC * 128, T], F32, "ExternalInput")
    W = C.dram("W", [KC * 128, N], F32, "ExternalInput")
    nw = C.dram("nw", [128, KC], F32, "ExternalInput") if norm else None
    gT = C.dram("gT", [KC * 128, T], F32, "ExternalInput") if gate else None
    resT = C.dram("resT", [N, T], F32, "ExternalInput") if resid else None
    od = {}
    for (kind, col0, ncols, dt, name) in outs:
        od[name] = C.dram(name, [ncols, T] if kind == 'f' else [T, ncols], dt, "ExternalOutput")
    if gla_gate:
        w1 = C.dram("w1", [KC * 128, 16], F32, "ExternalInput")
        w2e = C.dram("w2e", [17, 1024], F32, "ExternalInput")
        la = C.dram("la", [T, 1024], F32, "ExternalOutput")
    anT = C.sb([128, KC, T], BF16)
    nw_sb = None
    if norm:
        nw_sb = C.sb([128, KC], F32)
        S.issue('dsp', lambda e: e.dma_start(out=nw_sb[:], in_=nw), writes=['nw'])
    frontend(C, aT, KC, [(i * TT, TT) for i in range(T // TT)], anT, nw_sb, G if norm else None, gT)
    Wv = W.rearrange("(c p) n -> p c n", p=128)
    with contextlib.ExitStack() as st:
        wb = [C.sb([128, KC, 512], BF16, st) for _ in range(2)]
        stf = [C.sb([128, T], F32, st) for _ in range(2)]
        stt = [C.sb([128, 4, 512], F32, st) for _ in range(2)]
        sttb = [C.sb([128, 4, 512], BF16, st) for _ in range(2)]
        r_sb = [C.sb([128, TT], F32, st) for _ in range(4)]
        ps = C.ps([128, 6, 512], F32, st)
        cnt = dict(w=0, b=0, f=0, t=0, r=0, e=0)
        def evac(src, dst, reads, writes):
            cnt['e'] += 1
            if cnt['e'] % 2:
                S.issue('act', lambda e: e.activation(out=dst, in_=src, func=AF.Copy), reads=reads, writes=writes)
            else:
                S.issue('dve', lambda e: e.tensor_copy(out=dst, in_=src), reads=reads, writes=writes)
        if gla_gate:
            w1b = C.sb([128, KC, 16], BF16, st)
            rTe = C.sb([17, T], F32, st)
            w2s = C.sb([17, 1024], F32, st)
            S.issue('dpool', lambda e: e.dma_start(out=w1b[:], in_=w1.rearrange("(c p) n -> p c n", p=128)), writes=['w1b'])
            S.issue('dsp', lambda e: e.dma_start(out=w2s[:], in_=w2e), writes=['w2s'])
            S.issue('dve', lambda e: e.memset(rTe[:], 1.0), writes=['rTe'])
            for tt in range(T // TT):
                b = cnt['b'] % 6; cnt['b'] += 1
                for kc in range(KC):
                    S.issue('pe', lambda e: e.matmul(ps[0:16, b, :], lhsT=w1b[:, kc, :], rhs=anT[:, kc, tt * TT:(tt + 1) * TT],
                                                     start=(kc == 0), stop=(kc == KC - 1)), reads=['w1b'], writes=[('ps', b)])
                S.issue('act', lambda e: e.activation(out=rTe[0:16, tt * TT:(tt + 1) * TT], in_=ps[0:16, b, :], func=AF.Copy),
                        reads=[('ps', b)], writes=['rTe'])
            lav = la.rearrange("(i p) n -> p i n", p=128)
            for ti in range(T // 128):
                sl = (ti // 2) % 2
                for hf in range(2):
                    b = cnt['b'] % 6; cnt['b'] += 1
                    S.issue('pe', lambda e: e.matmul(ps[:, b, :], lhsT=rTe[:, ti * 128:(ti + 1) * 128], rhs=w2s[:, hf * 512:(hf + 1) * 512],
                                                     start=True, stop=True), reads=['rTe', 'w2s'], writes=[('ps', b)])
                    k = cnt['r'] % 4; cnt['r'] += 1
                    S.issue('act', lambda e: e.activation(out=r_sb[k][:], in_=ps[:, b, :], func=AF.Exp, scale=-1.0),
                            reads=[('ps', b)], writes=[('r_sb', k)])
                    S.issue('act', lambda e: e.activation(out=r_sb[k][:], in_=r_sb[k][:], func=AF.Ln, bias=1.0),
                            reads=[('r_sb', k)], writes=[('r_sb', k)])
                    S.issue('dve', lambda e: e.tensor_scalar(out=stt[sl][:, (ti % 2) * 2 + hf, :], in0=r_sb[k][:], scalar1=-1.0 / 16.0,
                                                             scalar2=None, op0=ALU.mult),
                            reads=[('r_sb', k)], writes=[('stt', sl)])
                if ti % 2 == 1:
                    S.issue('dsp', lambda e: e.dma_start(out=lav[:, ti - 1:ti + 1, :], in_=stt[sl][:].rearrange("p (i h) n -> p i (h n)", h=2)),
                            reads=[('stt', sl)], writes=[('la', ti)])
        for (kind, col0, ncols, dt, name) in outs:
            o = od[name]
            for cb in range(0, ncols, 512):
                wcols = min(512, ncols - cb)
                wi = cnt['w'] % 2; cnt['w'] += 1
                S.issue('dpool', lambda e: e.dma_start(out=wb[wi][:, :, 0:wcols], in_=Wv[:, :, col0 + cb:col0 + cb + wcols]),
                        writes=[('wb', wi)])
                if kind == 'f':
                    for j in range(wcols // 128):
                        fi = cnt['f'] % 2; cnt['f'] += 1
                        row0 = cb + j * 128
                        for tt in range(T // TT):
                            b = cnt['b'] % 6; cnt['b'] += 1
                            for kc in range(KC):
                                S.issue('pe', lambda e: e.matmul(ps[:, b, :], lhsT=wb[wi][:, kc, j * 128:(j + 1) * 128],
                                                                 rhs=anT[:, kc, tt * TT:(tt + 1) * TT], start=(kc == 0), stop=(kc == KC - 1)),
                                        reads=[('wb', wi)], writes=[('ps', b)])
                            if resid:
                                k = cnt['r'] % 4; cnt['r'] += 1
                                S.issue('dact', lambda e: e.dma_start(out=r_sb[k][:], in_=resT[col0 + row0:col0 + row0 + 128, tt * TT:(tt + 1) * TT]),
                                        writes=[('r_sb', k)])
                                S.issue('dve', lambda e: e.tensor_tensor(out=stf[fi][:, tt * TT:(tt + 1) * TT], in0=ps[:, b, :], in1=r_sb[k][:], op=ALU.add),
                                        reads=[('ps', b), ('r_sb', k)], writes=[('stf', fi)])
                            else:
                                evac(ps[:, b, :], stf[fi][:, tt * TT:(tt + 1) * TT], [('ps', b)], [('stf', fi)])
                        S.issue('dsp', lambda e: e.dma_start(out=o[row0:row0 + 128, :], in_=stf[fi][:]), reads=[('stf', fi)], writes=[(name, 'f', row0)])
                else:
                    ov = o.rearrange("(i p) n -> p i n", p=128)
                    stg = sttb if dt == BF16 else stt
                    sk = 'sttb' if dt == BF16 else 'stt'
                    for ti in range(T // 128):
                        ti_slot = cnt['t'] % 2
                        b = cnt['b'] % 6; cnt['b'] += 1
                        for kc in range(KC):
                            S.issue('pe', lambda e: e.matmul(ps[:, b, 0:wcols], lhsT=anT[:, kc, ti * 128:(ti + 1) * 128], rhs=wb[wi][:, kc, 0:wcols],
                                                             start=(kc == 0), stop=(kc == KC - 1)), reads=[('wb', wi)], writes=[('ps', b)])
                        evac(ps[:, b, 0:wcols], stg[ti_slot][:, ti % 4, 0:wcols], [('ps', b)], [(sk, ti_slot)])
                        if ti % 4 == 3:
                            S.issue('dsp', lambda e: e.dma_start(out=ov[:, ti - 3:ti + 1, cb:cb + wcols], in_=stg[ti_slot][:, :, 0:wcols]),
                                    reads=[(sk, ti_slot)], writes=[(name, 't', cb, ti)])
                            cnt['t'] += 1
    return C.done()

SEQ = 16384

def build_gla_rec():
    C = Ctx(); nc, S = C.nc, C.S
    qT = C.dram("qT", [256, SEQ], F32, "ExternalInput")
    kT = C.dram("kT", [256, SEQ], F32, "ExternalInput")
    kk = C.dram("k", [SEQ, 256], F32, "ExternalInput")
    la = C.dram("la", [SEQ, 256], F32, "ExternalInput")
    vv = C.dram("v", [SEQ, 256], BF16, "ExternalInput")
    Uc = C.dram("U", [128, 128], F32, "ExternalInput")
    Lc = C.dram("L", [128, 128], F32, "ExternalInput")
    oT = C.dram("oT", [256, SEQ], F32, "ExternalOutput")
    qTv = qT.rearrange("(c p) t -> p c t", p=128); kTv = kT.rearrange("(c p) t -> p c t", p=128)
    kv_ = kk.rearrange("(i p) n -> p i n", p=128); lav = la.rearrange("(i p) n -> p i n", p=128)
    vvv = vv.rearrange("(i p) n -> p i n", p=128); oTv = oT.rearrange("(c p) t -> p c t", p=128)
    U = C.sb([128, 128], F32); L = C.sb([128, 128], F32)
    S.issue('dsp', lambda e: e.dma_start(out=U[:], in_=Uc), writes=['U'])
    S.issue('dsp', lambda e: e.dma_start(out=L[:], in_=Lc), writes=['L'])
    q4 = [C.sb([128, 2, 512], F32) for _ in range(2)]
    k4T = [C.sb([128, 2, 512], F32) for _ in range(2)]
    k4 = [C.sb([128, 4, 256], F32) for _ in range(2)]
    la4 = [C.sb([128, 4, 256], F32) for _ in range(2)]
    v4 = [C.sb([128, 4, 256], BF16) for _ in range(2)]
    ost = [C.sb([128, 2, 512], F32) for _ in range(2)]
    Ep = [C.sb([128, 2, 128], F32) for _ in range(2)]
    Em = [C.sb([128, 2, 128], F32) for _ in range(2)]
    Er = [C.sb([128, 256], F32) for _ in range(2)]
    qd = [C.sb([128, 2, 128], BF16) for _ in range(2)]
    ki = [C.sb([128, 2, 128], BF16) for _ in range(2)]
    ke = [C.sb([128, 256], BF16) for _ in range(2)]
    am = [C.sb([128, 128], BF16) for _ in range(2)]
    St = C.sb([128, 2, 256], F32); Sb = C.sb([128, 2, 256], BF16)
    S.issue('dve', lambda e: e.memset(St[:], 0.0), writes=['St0', 'St1'])
    S.issue('dve', lambda e: e.memset(Sb[:], 0.0), writes=['Sb0', 'Sb1'])
    pb = [C.ps([128, 512], F32) for _ in range(2)]
    prem = C.ps([128, 512], F32); pat = C.ps([128, 512], F32)
    po = [C.ps([128, 512], F32) for _ in range(2)]
    pkv = [C.ps([128, 512], F32) for _ in range(2)]
    for g in range(SEQ // 512):
        gb = g % 2
        ts = slice(g * 512, (g + 1) * 512)
        S.issue('dsp', lambda e: e.dma_start(out=q4[gb][:], in_=qTv[:, :, ts]), writes=[('q4', gb)])
        S.issue('dsp', lambda e: e.dma_start(out=k4T[gb][:], in_=kTv[:, :, ts]), writes=[('k4T', gb)])
        S.issue('dact', lambda e: e.dma_start(out=k4[gb][:], in_=kv_[:, 4 * g:4 * g + 4, :]), writes=[('k4', gb)])
        S.issue('dact', lambda e: e.dma_start(out=la4[gb][:], in_=lav[:, 4 * g:4 * g + 4, :]), writes=[('la4', gb)])
        S.issue('dact', lambda e: e.dma_start(out=v4[gb][:], in_=vvv[:, 4 * g:4 * g + 4, :]), writes=[('v4', gb)])
        for i in range(4):
            n = 4 * g + i; b = n % 2
            cs = slice(i * 128, (i + 1) * 128)
            for kc in range(2):
                S.issue('pe', lambda e: e.matmul(pb[b][:, kc * 128:(kc + 1) * 128], lhsT=la4[gb][:, i, kc * 128:(kc + 1) * 128], rhs=U[:],
                                                 start=True, stop=True), reads=[('la4', gb), 'U'], writes=[('pb', b)])
            S.issue('pe', lambda e: e.matmul(prem[:, 0:256], lhsT=L[:], rhs=la4[gb][:, i, :], start=True, stop=True),
                    reads=[('la4', gb), 'L'], writes=['prem'])
            S.issue('act', lambda e: e.activation(out=Ep[b][:].rearrange("p c t -> p (c t)"), in_=pb[b][:, 0:256], func=AF.Exp),
                    reads=[('pb', b)], writes=[('Ep', b)])
            S.issue('act', lambda e: e.activation(out=Em[b][:].rearrange("p c t -> p (c t)"), in_=pb[b][:, 0:256], func=AF.Exp, scale=-1.0),
                    reads=[('pb', b)], writes=[('Em', b)])
            S.issue('act', lambda e: e.activation(out=Er[b][:], in_=prem[:, 0:256], func=AF.Exp),
                    reads=['prem'], writes=[('Er', b)])
            S.issue('dve', lambda e: e.scalar_tensor_tensor(out=qd[b][:], in0=q4[gb][:, :, cs], scalar=1.0 / 16.0, op0=ALU.mult,
                                                            in1=Ep[b][:], op1=ALU.mult),
                    reads=[('q4', gb), ('Ep', b)], writes=[('qd', b)])
            S.issue('dve', lambda e: e.tensor_tensor(out=ki[b][:], in0=k4T[gb][:, :, cs], in1=Em[b][:], op=ALU.mult),
                    reads=[('k4T', gb), ('Em', b)], writes=[('ki', b)])
            S.issue('pool', lambda e: e.tensor_tensor(out=ke[b][:], in0=k4[gb][:, i, :], in1=Er[b][:], op=ALU.mult),
                    reads=[('k4', gb), ('Er', b)], writes=[('ke', b)])
            for kc in range(2):
                S.issue('pe', lambda e: e.matmul(pat[:, 0:128], lhsT=ki[b][:, kc, :], rhs=qd[b][:, kc, :], start=(kc == 0), stop=(kc == 1)),
                        reads=[('ki', b), ('qd', b)], writes=['pat'])
            S.issue('dve', lambda e: e.tensor_tensor(out=am[b][:], in0=pat[:, 0:128], in1=U[:], op=ALU.mult),
                    reads=['pat', 'U'], writes=[('am', b)])
            for vc in range(2):
                S.issue('pe', lambda e: e.matmul(po[b][:, vc * 128:(vc + 1) * 128], lhsT=v4[gb][:, i, vc * 128:(vc + 1) * 128], rhs=am[b][:],
                                                 start=True, stop=False), reads=[('v4', gb), ('am', b)], writes=[('po', b)])
                for kc in range(2):
                    S.issue('pe', lambda e: e.matmul(po[b][:, vc * 128:(vc + 1) * 128], lhsT=Sb[:, kc, vc * 128:(vc + 1) * 128], rhs=qd[b][:, kc, :],
                                                     start=False, stop=(kc == 1)), reads=[f'Sb{kc}', ('qd', b)], writes=[('po', b)])
            S.issue('act', lambda e: e.activation(out=ost[gb][:, :, cs], in_=po[b][:, 0:256].rearrange("p (c t) -> p c t", c=2), func=AF.Copy),
                    reads=[('po', b)], writes=[('ost', gb)])
            for kc in range(2):
                S.issue('pe', lambda e: e.matmul(pkv[b][:, kc * 256:(kc + 1) * 256], lhsT=ke[b][:, kc * 128:(kc + 1) * 128], rhs=v4[gb][:, i, :],
                                                 start=True, stop=True), reads=[('ke', b), ('v4', gb)], writes=[('pkv', b)])
            for kc in range(2):
                S.issue('dve', lambda e: e.scalar_tensor_tensor(out=St[:, kc, :], in0=St[:, kc, :], scalar=Ep[b][:, kc, 127:128], op0=ALU.mult,
                                                                in1=pkv[b][:, kc * 256:(kc + 1) * 256], op1=ALU.add),
                        reads=[f'St{kc}', ('Ep', b), ('pkv', b)], writes=[f'St{kc}'])
                S.issue('pool', lambda e: e.tensor_copy(out=Sb[:, kc, :], in_=St[:, kc, :]), reads=[f'St{kc}'], writes=[f'Sb{kc}'])
        S.issue('dsp', lambda e: e.dma_start(out=oTv[:, :, ts], in_=ost[gb][:]), reads=[('ost', gb)], writes=[('oT', g)])
    return C.done()

FH = 5632
NFC = FH // 128

def build_ffn():
    C = Ctx(); nc, S = C.nc, C.S
    aTe = C.dram("aTe", [2048, T + 2], F32, "ExternalInput")
    nw = C.dram("nw", [128, 16], F32, "ExternalInput")
    Wup = C.dram("Wup", [2048, 2 * FH], F32, "ExternalInput")
    cwd = C.dram("cw", [128, 2 * NFC, 3], F32, "ExternalInput")
    cbd = C.dram("cb", [128, 2 * NFC], F32, "ExternalInput")
    Wdn = C.dram("Wdn", [FH, 2048], F32, "ExternalInput")
    hT = C.dram("hT", [2048, T], F32, "ExternalOutput")
    actT = C.dram("actT", [FH, T], BF16, "Internal")
    nw_sb = C.sb([128, 16], F32); cw = C.sb([128, 2 * NFC, 3], F32); cb = C.sb([128, 2 * NFC], F32)
    S.issue('dsp', lambda e: e.dma_start(out=nw_sb[:], in_=nw), writes=['nw'])
    S.issue('dsp', lambda e: e.dma_start(out=cw[:], in_=cwd), writes=['cw'])
    S.issue('dsp', lambda e: e.dma_start(out=cb[:], in_=cbd), writes=['cw'])
    Wuv = Wup.rearrange("(c p) n -> p c n", p=128)
    with contextlib.ExitStack() as st1:
        hnT = C.sb([128, 16, T + 2], BF16, st1)
        frontend(C, aTe, 16, [(0, 2)] + [(2 + i * TT, TT) for i in range(T // TT)], hnT, nw_sb, 16)
        wb = [C.sb([128, 16, 256], BF16, st1) for _ in range(2)]
        hbuf = [[C.sb([128, T + 2], F32, st1) for _ in range(2)] for _ in range(2)]
        y = [C.sb([128, T], F32, st1) for _ in range(2)]
        actb = [C.sb([128, T], BF16, st1) for _ in range(2)]
        ps = C.ps([128, 7, 512], F32, st1)
        ph = C.ps([128, 512], F32, st1)
        nb = 0; ne = 0
        for fc in range(NFC):
            wi = fc % 2
            S.issue('dpool', lambda e: e.dma_start(out=wb[wi][:, :, 0:128], in_=Wuv[:, :, fc * 128:(fc + 1) * 128]), writes=[('wb', wi)])
            S.issue('dpool', lambda e: e.dma_start(out=wb[wi][:, :, 128:256], in_=Wuv[:, :, FH + fc * 128:FH + (fc + 1) * 128]), writes=[('wb', wi)])
            for s in range(2):
                hb = hbuf[s][wi]; hk = ('hbuf', s, wi)
                for kc in range(16):
                    S.issue('pe', lambda e: e.matmul(ph[:, s * 8:s * 8 + 2], lhsT=wb[wi][:, kc, s * 128:(s + 1) * 128], rhs=hnT[:, kc, 0:2],
                                                     start=(kc == 0), stop=(kc == 15)), reads=[('wb', wi)], writes=['ph'])
                S.issue('act', lambda e: e.activation(out=hb[:, 0:2], in_=ph[:, s * 8:s * 8 + 2], func=AF.Copy), reads=['ph'], writes=[hk])
                for tt in range(T // TT):
                    b = nb % 7; nb += 1
                    for kc in range(16):
                        S.issue('pe', lambda e: e.matmul(ps[:, b, :], lhsT=wb[wi][:, kc, s * 128:(s + 1) * 128],
                                                         rhs=hnT[:, kc, 2 + tt * TT:2 + (tt + 1) * TT], start=(kc == 0), stop=(kc == 15)),
                                reads=[('wb', wi)], writes=[('ps', b)])
                    ne += 1
                    if ne % 2:
                        S.issue('act', lambda e: e.activation(out=hb[:, 2 + tt * TT:2 + (tt + 1) * TT], in_=ps[:, b, :], func=AF.Copy),
                                reads=[('ps', b)], writes=[hk])
                    else:
                        S.issue('dve', lambda e: e.tensor_copy(out=hb[:, 2 + tt * TT:2 + (tt + 1) * TT], in_=ps[:, b, :]),
                                reads=[('ps', b)], writes=[hk])
                ch = s * NFC + fc
                S.issue('act', lambda e: e.activation(out=y[s][:], in_=hb[:, 2:T + 2], func=AF.Identity, scale=cw[:, ch, 2:3], bias=cb[:, ch:ch + 1]),
                        reads=[hk, 'cw'], writes=[('y', s)])
                S.issue('dve', lambda e: e.scalar_tensor_tensor(out=y[s][:], in0=hb[:, 1:T + 1], scalar=cw[:, ch, 1:2], op0=ALU.mult, in1=y[s][:], op1=ALU.add),
                        reads=[hk, 'cw', ('y', s)], writes=[('y', s)])
                S.issue('dve', lambda e: e.scalar_tensor_tensor(out=y[s][:], in0=hb[:, 0:T], scalar=cw[:, ch, 0:1], op0=ALU.mult, in1=y[s][:], op1=ALU.add),
                        reads=[hk, 'cw', ('y', s)], writes=[('y', s)])
            S.issue('act', lambda e: e.activation(out=y[0][:], in_=y[0][:], func=AF.Silu), reads=[('y', 0)], writes=[('y', 0)])
            S.issue('dve', lambda e: e.tensor_tensor(out=actb[wi][:], in0=y[0][:], in1=y[1][:], op=ALU.mult),
                    reads=[('y', 0), ('y', 1)], writes=[('actb', wi)])
            S.issue('dsp', lambda e: e.dma_start(out=actT[fc * 128:(fc + 1) * 128, :], in_=actb[wi][:]), reads=[('actb', wi)], writes=[('actT', fc)])
        S.barrier()
    actv = actT.rearrange("(c p) t -> p c t", p=128)
    Wdv = Wdn.rearrange("(c p) n -> p c n", p=128)
    with contextlib.ExitStack() as st2:
        act_sb = C.sb([128, NFC, 1024], BF16, st2)
        wd = [C.sb([128, NFC, 256], BF16, st2) for _ in range(2)]
        r_sb = [C.sb([128, TT], F32, st2) for _ in range(4)]
        stg = [C.sb([128, 1024], F32, st2) for _ in range(2)]
        ps = C.ps([128, 6, 512], F32, st2)
        nb = 0; nr = 0; nw_ = 0; ns = 0
        for th in range(2):
            for q4 in range(4):
                S.issue('dsp', lambda e: e.dma_start(out=act_sb[:, q4 * 11:(q4 + 1) * 11, :], in_=actv[:, q4 * 11:(q4 + 1) * 11, th * 1024:(th + 1) * 1024]),
                        reads=[('actT', fcx) for fcx in range(q4 * 11, (q4 + 1) * 11)], writes=['act_sb'])
            for cb_ in range(8):
                wi = nw_ % 2; nw_ += 1
                S.issue('dpool', lambda e: e.dma_start(out=wd[wi][:], in_=Wdv[:, :, cb_ * 256:(cb_ + 1) * 256]), writes=[('wd', wi)])
                for j in range(2):
                    si = ns % 2; ns += 1
                    row0 = cb_ * 256 + j * 128
                    for t2 in range(2):
                        b = nb % 6; nb += 1
                        for fcc in range(NFC):
                            S.issue('pe', lambda e: e.matmul(ps[:, b, :], lhsT=wd[wi][:, fcc, j * 128:(j + 1) * 128], rhs=act_sb[:, fcc, t2 * 512:(t2 + 1) * 512],
                                                             start=(fcc == 0), stop=(fcc == NFC - 1)), reads=[('wd', wi), 'act_sb'], writes=[('ps', b)])
                        k = nr % 4; nr += 1
                        c0 = 2 + th * 1024 + t2 * 512
                        S.issue('dact', lambda e: e.dma_start(out=r_sb[k][:], in_=aTe[row0:row0 + 128, c0:c0 + 512]), writes=[('r_sb', k)])
                        S.issue('dve', lambda e: e.tensor_tensor(out=stg[si][:, t2 * 512:(t2 + 1) * 512], in0=ps[:, b, :], in1=r_sb[k][:], op=ALU.add),
                                reads=[('ps', b), ('r_sb', k)], writes=[('stg', si)])
                    S.issue('dsp', lambda e: e.dma_start(out=hT[row0:row0 + 128, th * 1024:(th + 1) * 1024], in_=stg[si][:]), reads=[('stg', si)], writes=[('hT', th, row0)])
    return C.done()

def build_glapre():
    return build_normproj(16, 6144, [('f', 0, 1024, F32, 'qT'), ('f', 1024, 1024, F32, 'kT'), ('f', 4096, 2048, F32, 'ggT'),
                                     ('t', 1024, 1024, F32, 'k'), ('t', 2048, 2048, BF16, 'v')], norm=True, G=16, gla_gate=True)
def build_glapost():
    return build_normproj(16, 2048, [('f', 0, 2048, F32, 'hT')], norm=True, G=4, gate=True, resid=True)
def build_kvproj():
    return build_normproj(16, 4096, [('f', 0, 2048, F32, 'KT'), ('t', 2048, 2048, BF16, 'V')], norm=True, G=16)
def build_qproj():
    return build_normproj(16, 2048, [('f', 0, 2048, F32, 'QT')], norm=True, G=16)
def build_mobaout():
    return build_normproj(16, 2048, [('f', 0, 2048, F32, 'hT')], norm=False, resid=True)

NBLK = 64

def build_moba(nheads=2, nqg=SEQ // 512, ntiles=SEQ // 512, do_gate=True, skip=()):
    C = Ctx(); nc, S = C.nc, C.S
    QT = C.dram("QT", [256, SEQ], F32, "ExternalInput")
    KT = C.dram("KT", [256, SEQ], F32, "ExternalInput")
    V = C.dram("V", [SEQ, 256], BF16, "ExternalInput")
    qwd = C.dram("qw", [128, 2], F32, "ExternalInput")
    kwd = C.dram("kw", [128, 2], F32, "ExternalInput")
    c31d = C.dram("c31", [128, 2], F32, "ExternalInput")
    Tbd = C.dram("Tb", [128, 2, 6, 512], F32, "ExternalInput")
    identd = C.dram("ident", [128, 128], F32, "ExternalInput")
    seld = C.dram("sel", [128, 64, 128], BF16, "ExternalInput")
    oT = C.dram("oT", [256, SEQ], F32, "ExternalOutput")
    Vv = V.rearrange("(i p) n -> p i n", p=128)
    qw = C.sb([128, 2], F32); kw = C.sb([128, 2], F32); c31 = C.sb([128, 2], F32); nc31 = C.sb([128, 2], F32)
    ident = C.sb([128, 128], F32); sel = C.sb([128, 64, 128], BF16)
    ones_b = C.sb([128, 128], BF16)
    for dst, src, k in ((qw, qwd, 'qw'), (kw, kwd, 'kw'), (c31, c31d, 'c31'), (ident, identd, 'ident'), (sel, seld, 'sel')):
        S.issue('dsp', lambda e: e.dma_start(out=dst[:], in_=src), writes=[k])
    S.issue('dve', lambda e: e.memset(ones_b[:], 1.0), writes=['ones_b'])
    S.issue('dve', lambda e: e.tensor_scalar(out=nc31[:], in0=c31[:], scalar1=-1.0, scalar2=None, op0=ALU.mult), reads=['c31'], writes=['nc31'])
    QnT = C.sb([128, SEQ], BF16); KnT = C.sb([128, SEQ], BF16); Vb = C.sb([128, 128, 128], BF16)
    mbT = C.sb([128, SEQ], BF16); ET = C.sb([128, 6, 512], F32)
    S.issue('pool', lambda e: e.memset(mbT[64:128, :], 0.0), writes=['mbT'])
    for hh in range(nheads):
        rows = slice(hh * 128, (hh + 1) * 128)
        with contextlib.ExitStack() as st:
            xin = [C.sb([128, 512], F32, st) for _ in range(2)]
            sq = [C.sb([128, 512], BF16, st) for _ in range(2)]
            rt = [C.sb([128, 512], F32, st) for _ in range(2)]
            nf = [C.sb([128, 512], F32, st) for _ in range(2)]
            kmean = C.sb([128, NBLK], F32, st)
            gs = [C.sb([128, NBLK], F32, st) for _ in range(2)]
            m8 = [C.sb([128, 8], F32, st) for _ in range(2)]
            mb = [C.sb([128, 4, NBLK], F32, st) for _ in range(2)]
            pss = C.ps([128, 2, 512], F32, st)
            pg = C.ps([128, 4, 128], F32, st)
            pmt = C.ps([128, 2, 512], F32, st)
            S.issue('dsp', lambda e: e.dma_start(out=ET[:], in_=Tbd[:, hh, :, :]), writes=['ET'])
            if 'et' not in skip:
                S.issue('act', lambda e: e.activation(out=ET[:], in_=ET[:], func=AF.Exp, bias=nc31[:, hh:hh + 1]), reads=['ET', 'nc31'], writes=['ET'])
            for half in range(0 if 'vb' in skip else 8):
                S.issue('dact', lambda e: e.dma_start(out=Vb[:, half * 16:(half + 1) * 16, :], in_=Vv[:, half * 16:(half + 1) * 16, rows]), writes=['Vb'])
            for which in range(2):
                src = KT if which == 0 else QT
                wv = kw if which == 0 else qw
                for t in range(ntiles):
                    b = t % 2
                    ts = slice(t * 512, (t + 1) * 512)
                    S.issue('dsp', lambda e: e.dma_start(out=xin[b][:], in_=src[rows, ts]), writes=[('xin', b)])
                    S.issue('act', lambda e: e.activation(out=sq[b][:], in_=xin[b][:], func=AF.Square), reads=[('xin', b)], writes=[('sq', b)])
                    S.issue('pe', lambda e: e.matmul(pss[:, b, :], lhsT=ones_b[:], rhs=sq[b][:], start=True, stop=True),
                            reads=['ones_b', ('sq', b)], writes=[('pss', b)])
                    S.issue('act', lambda e: e.activation(out=rt[b][:], in_=pss[:, b, :], func=AF.Sqrt, scale=1.0 / 128, bias=EPS),
                            reads=[('pss', b)], writes=[('rt', b)])
                    S.issue('dve', lambda e: e.reciprocal(out=rt[b][:], in_=rt[b][:]), reads=[('rt', b)], writes=[('rt', b)])
                    S.issue('dve', lambda e: e.scalar_tensor_tensor(out=nf[b][:], in0=xin[b][:], scalar=wv[:, hh:hh + 1], op0=ALU.mult,
                                                                    in1=rt[b][:], op1=ALU.mult),
                            reads=[('xin', b), ('rt', b), 'qw', 'kw'], writes=[('nf', b)])
                    if which == 0:
                        S.issue('act', lambda e: e.activation(out=KnT[:, ts], in_=nf[b][:], func=AF.Copy), reads=[('nf', b)], writes=['KnT'])
                        if 'km' not in skip: S.issue('dve', lambda e: e.tensor_reduce(out=kmean[:, 2 * t:2 * t + 2], in_=nf[b][:].rearrange("p (a c) -> p a c", a=2),
                                                                 axis=mybir.AxisListType.X, op=ALU.add), reads=[('nf', b)], writes=['kmean'])
                    else:
                        S.issue('act', lambda e: e.activation(out=QnT[:, ts], in_=nf[b][:], func=AF.Copy, scale=128.0 ** -0.5),
                                reads=[('nf', b)], writes=['QnT'])
                        for i in range(4):
                            qt = 4 * t + i; own = qt // 2
                            if own >= 3 and do_gate:
                                S.issue('pe', lambda e: e.matmul(pg[:, i, 0:NBLK], lhsT=nf[b][:, i * 128:(i + 1) * 128], rhs=kmean[:], start=True, stop=True),
                                        reads=[('nf', b), 'kmean'], writes=['pg'])
                                g2 = i % 2
                                S.issue('pool', lambda e: e.memset(gs[g2][:], -1e30), writes=[('gs', g2)])
                                S.issue('dve', lambda e: e.tensor_copy(out=gs[g2][:, 0:own], in_=pg[:, i, 0:own]), reads=['pg'], writes=[('gs', g2)])
                                if 'mx' not in skip: S.issue('dve', lambda e: e.max(out=m8[g2][:], in_=gs[g2][:]), reads=[('gs', g2)], writes=[('m8', g2)])
                                else: S.issue('dve', lambda e: e.tensor_copy(out=m8[g2][:], in_=gs[g2][:, 0:8]), reads=[('gs', g2)], writes=[('m8', g2)])
                                if 'ts' not in skip: S.issue('dve', lambda e: e.tensor_scalar(out=gs[g2][:], in0=gs[g2][:], scalar1=m8[g2][:, 2:3], op0=ALU.is_ge,
                                                                         scalar2=30000.0, op1=ALU.mult), reads=[('gs', g2), ('m8', g2)], writes=[('gs', g2)])
                                S.issue('dve', lambda e: e.tensor_scalar(out=mb[b][:, i, :], in0=gs[g2][:], scalar1=-30000.0, scalar2=None, op0=ALU.add),
                                        reads=[('gs', g2)], writes=[('mb', b)])
                                S.issue('dve', lambda e: e.memset(mb[b][:, i, own:own + 1], 0.0), writes=[('mb', b)])
                            else:
                                S.issue('dve', lambda e: e.memset(mb[b][:, i, :], 0.0), writes=[('mb', b)])
                        for i in range(0 if 'tr' in skip else 4):
                            S.issue('pe', lambda e: e.transpose(pmt[0:64, b, i * 128:(i + 1) * 128], mb[b][:, i, :], ident[:]),
                                    reads=[('mb', b), 'ident'], writes=[('pmt', b)])
                        if 'tr' not in skip: S.issue('act', lambda e: e.activation(out=mbT[0:64, ts], in_=pmt[0:64, b, :], func=AF.Copy), reads=[('pmt', b)], writes=['mbT'])
                if which == 0:
                    S.issue('dve', lambda e: e.tensor_scalar(out=kmean[:], in0=kmean[:], scalar1=1.0 / 256, scalar2=None, op0=ALU.mult),
                            reads=['kmean'], writes=['kmean'])
            S.barrier(new_sems=True)
        with contextlib.ExitStack() as st:
            NPT = 4
            PT = [C.sb([128, 2, 512], BF16, st) for _ in range(NPT)]
            Pf = [C.sb([128, 2, 512], F32, st) for _ in range(2)]
            rinv = [C.sb([128, 512], F32, st) for _ in range(2)]
            osb = [C.sb([128, 512], F32, st) for _ in range(2)]
            pst = [C.ps([128, 2, 512], F32, st) for _ in range(2)]
            po = [C.ps([128, 512], F32, st) for _ in range(2)]
            pden = [C.ps([128, 512], F32, st) for _ in range(2)]
            steps = [(qg, kp, h2) for qg in range(nqg) for kp in range(qg + 1) for h2 in range(2)]
            npf = [0]
            def emit_qk(idx):
                qg, kp, h2 = steps[idx]
                sb_ = idx % 2; k = idx % NPT
                qs = slice(qg * 512, (qg + 1) * 512)
                n = 2 * kp + h2
                for cc in range(2):
                    kc = kp * 4 + h2 * 2 + cc
                    S.issue('pe', lambda e: e.matmul(pst[sb_][:, cc, :], lhsT=KnT[:, kc * 128:(kc + 1) * 128], rhs=QnT[:, qs], start=True, stop=False),
                            reads=['KnT', 'QnT'], writes=[('pst', sb_)])
                    S.issue('pe', lambda e: e.matmul(pst[sb_][:, cc, :], lhsT=sel[:, n, :], rhs=mbT[:, qs], start=False, stop=True),
                            reads=['sel', 'mbT'], writes=[('pst', sb_)])
                pat = None
                if kp == qg: pat = h2 * 2
                elif kp == qg - 1 and h2 == 1: pat = 4
                if pat is None:
                    S.issue('act', lambda e: e.activation(out=PT[k][:], in_=pst[sb_][:], func=AF.Exp), reads=[('pst', sb_)], writes=[('PT', k)])
                else:
                    f = npf[0] % 2; npf[0] += 1
                    patsl = slice(4, 6) if pat == 4 else slice(pat, pat + 2)
                    S.issue('act', lambda e: e.activation(out=Pf[f][:], in_=pst[sb_][:], func=AF.Exp), reads=[('pst', sb_)], writes=[('Pf', f)])
                    S.issue('dve', lambda e: e.tensor_tensor(out=PT[k][:], in0=Pf[f][:], in1=ET[:, patsl, :], op=ALU.mult),
                            reads=[('Pf', f), 'ET'], writes=[('PT', k)])
            def emit_pv(idx):
                qg, kp, h2 = steps[idx]
                k = idx % NPT; ob = qg % 2
                qs = slice(qg * 512, (qg + 1) * 512)
                for cc in range(2):
                    kc = kp * 4 + h2 * 2 + cc
                    first = (kp == 0 and h2 == 0 and cc == 0); last = (kp == qg and h2 == 1 and cc == 1)
                    S.issue('pe', lambda e: e.matmul(po[ob][:], lhsT=Vb[:, kc, :], rhs=PT[k][:, cc, :], start=first, stop=last),
                            reads=['Vb', ('PT', k)], writes=[('po', ob)])
                    S.issue('pe', lambda e: e.matmul(pden[ob][:], lhsT=ones_b[:], rhs=PT[k][:, cc, :], start=first, stop=last),
                            reads=['ones_b', ('PT', k)], writes=[('pden', ob)])
                if kp == qg and h2 == 1:
                    S.issue('dve', lambda e: e.reciprocal(out=rinv[ob][:], in_=pden[ob][:]), reads=[('pden', ob)], writes=[('rinv', ob)])
                    S.issue('dve', lambda e: e.tensor_tensor(out=osb[ob][:], in0=po[ob][:], in1=rinv[ob][:], op=ALU.mult),
                            reads=[('po', ob), ('rinv', ob)], writes=[('osb', ob)])
                    S.issue('dsp', lambda e: e.dma_start(out=oT[rows, qs], in_=osb[ob][:]), reads=[('osb', ob)], writes=[('oT', hh, qg)])
            for idx in range(len(steps)):
                emit_qk(idx)
                if idx >= 1: emit_pv(idx - 1)
            if steps: emit_pv(len(steps) - 1)
            S.barrier(new_sems=True)
    return C.done()

def _rel_bucket_np(dist):
    n = np.maximum(dist, 0)
    with np.errstate(divide='ignore'):
        large = 16 + (np.log(np.maximum(n, 1).astype(np.float32) / 16) / np.float32(np.log(128 / 16)) * 16).astype(np.int32)
    large = np.minimum(large, 31)
    return np.where(n < 16, n, large)

def moba_consts(rel_bias, heads):
    q = np.arange(512)[None, :]
    kk = np.arange(128)[:, None]
    Tb = np.zeros((128, 2, 6, 512), np.float32)
    for p in range(6):
        koff = (p * 128) if p < 4 else ((p - 4 + 2) * 128 - 512)
        kpos = koff + kk
        dist = q - kpos
        qblk = q // 256; kblk = np.floor_divide(kpos, 256)
        valid = (dist >= 0)
        idx = _rel_bucket_np(dist)
        for hi, h in enumerate(heads):
            Tb[:, hi, p, :] = np.where(valid, rel_bias[idx, h], np.float32(-30000.0))
    return Tb

_PROGS = {}
def _prog(name, fn):
    if name not in _PROGS:
        _PROGS[name] = fn()
    return _PROGS[name]

def _ca(a): return np.ascontiguousarray(a)
def _nwl(v): return _ca(np.asarray(v, np.float32).reshape(-1, 128).T)

def _ffn_launch(hT_shards, lay, ffn_norm, ffn_w_up, ffn_conv_w, ffn_conv_b, ffn_w_down):
    cw = _ca(np.asarray(ffn_conv_w[lay], np.float32).reshape(3, 2 * NFC, 128).transpose(2, 1, 0))
    cb = _ca(np.asarray(ffn_conv_b[lay], np.float32).reshape(2 * NFC, 128).T)
    ims = []
    for c in range(8):
        a = np.zeros((2048, T + 2), np.float32)
        a[:, 2:] = hT_shards[c]
        if c > 0: a[:, :2] = hT_shards[c - 1][:, T - 2:]
        ims.append(dict(aTe=a, nw=_nwl(ffn_norm[lay]), Wup=_ca(np.asarray(ffn_w_up[lay], np.float32)), cw=cw, cb=cb,
                        Wdn=_ca(np.asarray(ffn_w_down[lay], np.float32))))
    r = run(_prog('ffn', build_ffn), ims)
    return [r[c]['hT'] for c in range(8)]

def kernel(x, gla_norm, gla_w_in, gla_gk_w1, gla_gk_w2, gla_gk_b, gla_o_norm, gla_w_out,
           kv_norm, kv_w, k_norm_w, moba_norm, moba_w_q, moba_q_norm, moba_w_out, rel_bias,
           ffn_norm, ffn_w_up, ffn_conv_w, ffn_conv_b, ffn_w_down):
    import ml_dtypes
    f = lambda a: np.asarray(a, np.float32)
    x = f(x)[0]
    xT = [_ca(x[c * T:(c + 1) * T].T) for c in range(8)]
    w2e = _ca(np.concatenate([f(gla_gk_w2)[0], f(gla_gk_b)[0][None]], 0))
    ims = [dict(aT=xT[c], W=_ca(f(gla_w_in)[0]), nw=_nwl(f(gla_norm)[0]), w1=_ca(f(gla_gk_w1)[0]), w2e=w2e) for c in range(8)]
    r1 = run(_prog('glapre', build_glapre), ims)
    qT = np.concatenate([r['qT'] for r in r1], 1); kT = np.concatenate([r['kT'] for r in r1], 1)
    kk = np.concatenate([r['k'] for r in r1], 0); la = np.concatenate([r['la'] for r in r1], 0)
    vv = np.concatenate([r['v'] for r in r1], 0)
    ggT = [r['ggT'] for r in r1]
    del r1
    U = np.triu(np.ones((128, 128), np.float32)); L = np.tril(np.ones((128, 128), np.float32), -1)
    ims = []
    for c in range(8):
        h, j = c // 2, c % 2
        ims.append(dict(qT=_ca(qT[h * 256:(h + 1) * 256]), kT=_ca(kT[h * 256:(h + 1) * 256]), k=_ca(kk[:, h * 256:(h + 1) * 256]),
                        la=_ca(la[:, h * 256:(h + 1) * 256]), v=_ca(vv[:, h * 512 + j * 256:h * 512 + (j + 1) * 256]), U=U, L=L))
    r2 = run(_prog('glarec', build_gla_rec), ims)
    oT = np.concatenate([r['oT'] for r in r2], 0)
    del r2, qT, kT, kk, la, vv
    onw = _ca(np.tile(f(gla_o_norm)[0].reshape(4, 128).T, (1, 4)))
    ims = [dict(aT=_ca(oT[:, c * T:(c + 1) * T]), W=_ca(f(gla_w_out)[0]), nw=onw, gT=ggT[c], resT=xT[c]) for c in range(8)]
    r3 = run(_prog('glapost', build_glapost), ims)
    h1 = [r['hT'] for r in r3]
    del r3, oT, ggT
    h2 = _ffn_launch(h1, 0, ffn_norm, ffn_w_up, ffn_conv_w, ffn_conv_b, ffn_w_down)
    ims = [dict(aT=h2[c], W=_ca(f(kv_w)), nw=_nwl(f(kv_norm))) for c in range(8)]
    r5 = run(_prog('kvproj', build_kvproj), ims)
    KT = np.concatenate([r['KT'] for r in r5], 1); V = np.concatenate([r['V'] for r in r5], 0)
    del r5
    ims = [dict(aT=h2[c], W=_ca(f(moba_w_q)[0]), nw=_nwl(f(moba_norm)[0])) for c in range(8)]
    r5 = run(_prog('qproj', build_qproj), ims)
    QT = np.concatenate([r['QT'] for r in r5], 1)
    del r5
    rb = f(rel_bias)
    ident = np.eye(128, dtype=np.float32)
    sel = np.zeros((128, 64, 128), np.float32)
    for n in range(64): sel[n, n, :] = 1
    sel = sel.astype(ml_dtypes.bfloat16)
    ims = []
    for c in range(8):
        heads = [2 * c, 2 * c + 1]
        ims.append(dict(QT=_ca(QT[c * 256:(c + 1) * 256]), KT=_ca(KT[c * 256:(c + 1) * 256]), V=_ca(V[:, c * 256:(c + 1) * 256]),
                        qw=_ca(np.stack([f(moba_q_norm)[0]] * 2, 1)), kw=_ca(np.stack([f(k_norm_w)] * 2, 1)),
                        c31=_ca(np.broadcast_to(rb[31, heads][None, :], (128, 2))), Tb=moba_consts(rb, heads), ident=ident, sel=sel))
    r6 = run(_prog('moba', build_moba), ims)
    aoT = np.concatenate([r['oT'] for r in r6], 0)
    del r6, QT, KT, V
    ims = [dict(aT=_ca(aoT[:, c * T:(c + 1) * T]), W=_ca(f(moba_w_out)[0]), resT=h2[c]) for c in range(8)]
    r7 = run(_prog('mobaout', build_mobaout), ims)
    h3 = [r['hT'] for r in r7]
    del r7, aoT
    h4 = _ffn_launch(h3, 1, ffn_norm, ffn_w_up, ffn_conv_w, ffn_conv_b, ffn_w_down)
    out = np.concatenate([h.T for h in h4], 0)[None]
    return np.ascontiguousarray(out.astype(np.float32))
```
